# Optimizing a Trainium2 kernel written in Bass

```python
import math
import jax, jax.numpy as jnp
from jax import lax
import numpy as np

D_MODEL = 2048
BATCH = 4
SEQ = 2048
DEPTH = 1
DEC_BATCH = 128
DEC_SEQ = 8
PAST_LEN = 16384
PAGE_SIZE = 128

RET_WIDTH = D_MODEL // 2
SGU_WIDTH = D_MODEL - RET_WIDTH
RET_HEADS = 8
RET_DK = RET_WIDTH // RET_HEADS
RET_DV = RET_WIDTH // RET_HEADS
SGU_GROUPS = 8
SGU_CH = SGU_WIDTH // SGU_GROUPS
CHUNK = 128
D_FF = 4 * D_MODEL
IN_WIDTH = 4 * RET_WIDTH + 2 * SGU_WIDTH
ROPE_THETA = 10000.0
EPS = 1e-6

kernel_name = "hybrid_retention_sgu_decoder_step"


def rmsnorm(x, g):
    xf = x.astype(jnp.float32)
    y = xf * lax.rsqrt(jnp.mean(xf * xf, axis=-1, keepdims=True) + EPS) * g.astype(jnp.float32)
    return y.astype(x.dtype)


def layernorm(x, g, b):
    xf = x.astype(jnp.float32)
    mu = jnp.mean(xf, axis=-1, keepdims=True)
    var = jnp.mean(jnp.square(xf - mu), axis=-1, keepdims=True)
    y = (xf - mu) * lax.rsqrt(var + EPS)
    if g is not None:
        y = y * g.astype(jnp.float32) + b.astype(jnp.float32)
    return y.astype(x.dtype)


def rotary(x, pos):
    d = x.shape[-1]
    inv = 1.0 / (ROPE_THETA ** (jnp.arange(0, d, 2, dtype=jnp.float32) / d))
    ang = pos.astype(jnp.float32)[:, None] * inv[None, :]
    cos = jnp.cos(ang)[None, :, None, :]
    sin = jnp.sin(ang)[None, :, None, :]
    xf = x.astype(jnp.float32)
    x1, x2 = xf[..., : d // 2], xf[..., d // 2:]
    return jnp.concatenate([x1 * cos - x2 * sin, x1 * sin + x2 * cos], axis=-1)


def retention(q, k, v, s0, clen):
    B, L, H, _ = q.shape
    n = L // clen

    def chunks(t):
        return t.astype(jnp.float32).reshape(B, n, clen, H, t.shape[-1]).transpose(1, 0, 3, 2, 4)

    qc, kc, vc = chunks(q), chunks(k), chunks(v)
    lg = jnp.log(1.0 - jnp.power(2.0, -5.0 - jnp.arange(H, dtype=jnp.float32)))
    idx = jnp.arange(clen, dtype=jnp.float32)
    diff = idx[:, None] - idx[None, :]
    dmask = jnp.where(diff[None] >= 0, jnp.exp(jnp.maximum(diff, 0.0)[None] * lg[:, None, None]), 0.0)
    cross = jnp.exp((idx + 1.0)[None, :] * lg[:, None])
    sdec = jnp.exp((clen - 1.0 - idx)[None, :] * lg[:, None])
    cdec = jnp.exp(clen * lg)

    def step(S, inp):
        qb, kb, vb = inp
        scores = jnp.einsum('bhnd,bhmd->bhnm', qb, kb) * dmask[None]
        o = jnp.einsum('bhnm,bhme->bhne', scores, vb) \
            + jnp.einsum('bhnd,bhde->bhne', qb, S) * cross[None, :, :, None]
        S_new = S * cdec[None, :, None, None] \
            + jnp.einsum('bhmd,bhme->bhde', kb * sdec[None, :, :, None], vb)
        return S_new, o

    S, o = lax.scan(step, s0.astype(jnp.float32), (qc, kc, vc))
    o = o.transpose(1, 0, 3, 2, 4).reshape(B, L, H, vc.shape[-1])
    return o, S


def spatial_gate(vn, w_s, b_s, clen):
    B, L, _ = vn.shape
    n = L // clen
    vr = vn.reshape(B, n, clen, SGU_GROUPS, SGU_CH)
    tri = jnp.tril(jnp.ones((clen, clen), dtype=w_s.dtype))
    w = w_s[:, :clen, :clen] * tri[None]
    s = jnp.einsum('gts,bnsgc->bntgc', w, vr) + b_s[:, :clen].T[None, None, :, :, None]
    return s.reshape(B, L, SGU_WIDTH)


def hybrid_layer(x, c, pos, s0, w_ada, b_ada, g_pre_mix, g_post_mix, g_pre_ffn, g_post_ffn,
                 w_in, w_s, b_s, ln_g, ln_b, w_o, w_ff1, w_ff2):
    B, L, _ = x.shape
    dt = x.dtype
    mod = (jax.nn.silu(c.astype(jnp.float32)) @ w_ada.astype(jnp.float32) + b_ada.astype(jnp.float32)).astype(dt)
    sh1, sc1, gt1, sh2, sc2, gt2 = jnp.split(mod[:, None, :], 6, axis=-1)

    h = rmsnorm(x, g_pre_mix) * (1 + sc1) + sh1
    z = h @ w_in
    q, k, v, g, u, vs = jnp.split(z, np.cumsum([RET_WIDTH] * 4 + [SGU_WIDTH])[:].tolist(), axis=-1)
    q = rotary(q.reshape(B, L, RET_HEADS, RET_DK), pos)
    k = rotary(k.reshape(B, L, RET_HEADS, RET_DK), pos) * (RET_DK ** -0.5)
    v = v.reshape(B, L, RET_HEADS, RET_DV)
    clen = min(CHUNK, L)
    o, S = retention(q, k, v, s0, clen)
    o = layernorm(o, None, None).astype(dt).reshape(B, L, RET_WIDTH)
    ret_out = jax.nn.silu(g) * o

    u = jax.nn.gelu(u)
    vn = layernorm(jax.nn.gelu(vs), ln_g, ln_b)
    sgu_out = u * spatial_gate(vn, w_s, b_s, clen)

    m = jnp.concatenate([ret_out, sgu_out], axis=-1) @ w_o
    x = x + gt1 * rmsnorm(m, g_post_mix)

    h2 = rmsnorm(x, g_pre_ffn) * (1 + sc2) + sh2
    f = jnp.square(jax.nn.relu(h2 @ w_ff1)) @ w_ff2
    x = x + gt2 * rmsnorm(f, g_post_ffn)
    return x, S, vn


def setup_inputs(seed: int = 0) -> dict:
    key = jax.random.key(seed)
    ks = jax.random.split(key, 24)
    f32 = jnp.float32
    nrm = lambda k, shape, s: jax.random.normal(k, shape, f32) * s
    return {
        "x_prompt": nrm(ks[0], (BATCH, SEQ, D_MODEL), 1.0),
        "x_sample": nrm(ks[1], (DEC_BATCH, DEC_SEQ, D_MODEL), 1.0),
        "state_ret": nrm(ks[2], (DEPTH, DEC_BATCH, RET_HEADS, RET_DK, RET_DV), 0.1),
        "c_prompt": nrm(ks[3], (BATCH, D_MODEL), 1.0),
        "c_sample": nrm(ks[4], (DEC_BATCH, D_MODEL), 1.0),
        "w_ada": nrm(ks[5], (DEPTH, D_MODEL, 6 * D_MODEL), D_MODEL ** -0.5),
        "b_ada": nrm(ks[6], (DEPTH, 6 * D_MODEL), 0.02),
        "g_pre_mix": 1.0 + nrm(ks[7], (DEPTH, D_MODEL), 0.02),
        "g_post_mix": 1.0 + nrm(ks[8], (DEPTH, D_MODEL), 0.02),
        "g_pre_ffn": 1.0 + nrm(ks[9], (DEPTH, D_MODEL), 0.02),
        "g_post_ffn": 1.0 + nrm(ks[10], (DEPTH, D_MODEL), 0.02),
        "w_in": nrm(ks[11], (DEPTH, D_MODEL, IN_WIDTH), D_MODEL ** -0.5),
        "w_s": nrm(ks[12], (DEPTH, SGU_GROUPS, CHUNK, CHUNK), 0.5 * CHUNK ** -0.5),
        "b_s": 1.0 + nrm(ks[13], (DEPTH, SGU_GROUPS, CHUNK), 0.02),
        "ln_g": 1.0 + nrm(ks[14], (DEPTH, SGU_WIDTH), 0.02),
        "ln_b": nrm(ks[15], (DEPTH, SGU_WIDTH), 0.02),
        "w_o": nrm(ks[16], (DEPTH, RET_WIDTH + SGU_WIDTH, D_MODEL), (RET_WIDTH + SGU_WIDTH) ** -0.5),
        "w_ff1": nrm(ks[17], (DEPTH, D_MODEL, D_FF), D_MODEL ** -0.5),
        "w_ff2": nrm(ks[18], (DEPTH, D_FF, D_MODEL), D_FF ** -0.5),
    }


def reference(x_prompt, x_sample, state_ret, c_prompt, c_sample, w_ada, b_ada, g_pre_mix,
              g_post_mix, g_pre_ffn, g_post_ffn, w_in, w_s, b_s, ln_g, ln_b, w_o, w_ff1, w_ff2):
    pos_prompt = jnp.arange(SEQ, dtype=jnp.int32)
    pos_sample = PAST_LEN + jnp.arange(DEC_SEQ, dtype=jnp.int32)
    yp, ys = x_prompt, x_sample
    sp_list, ss_list, vs_list = [], [], []
    for l in range(DEPTH):
        w = (w_ada[l], b_ada[l], g_pre_mix[l], g_post_mix[l], g_pre_ffn[l], g_post_ffn[l],
             w_in[l], w_s[l], b_s[l], ln_g[l], ln_b[l], w_o[l], w_ff1[l], w_ff2[l])
        s0_prompt = jnp.zeros((BATCH, RET_HEADS, RET_DK, RET_DV), jnp.float32)
        yp, sp, _ = hybrid_layer(yp, c_prompt, pos_prompt, s0_prompt, *w)
        ys, ss, vn_s = hybrid_layer(ys, c_sample, pos_sample, state_ret[l], *w)
        sp_list.append(sp)
        ss_list.append(ss)
        vs_list.append(vn_s)
    state_ret_prompt = jnp.stack(sp_list)
    state_ret_sample = jnp.stack(ss_list)
    sgu_v_sample = jnp.stack(vs_list)
    return (yp, ys, state_ret_prompt, state_ret_sample, sgu_v_sample)
```

```python
import contextlib
import numpy as np
import concourse.bass as bass
import concourse.mybir as mybir
from concourse.bass_utils import run_bass_kernel_spmd

F32 = mybir.dt.float32
BF16 = mybir.dt.bfloat16
AF = mybir.ActivationFunctionType
ALU = mybir.AluOpType
AX = mybir.AxisListType

D = 2048
NCH = 9
T = NCH * 128
EPS = 1e-6
NSLOT = 5
SLOT_B = 16384
ARENA_B = 212800
OFF_RING = 0
OFF_ACT = NSLOT * SLOT_B
OFF_BIG = OFF_ACT + 36864
BIG_B = 40960
OFF_CONST = OFF_BIG + BIG_B
GAM = [1.0 - 2.0 ** (-5.0 - h) for h in range(8)]


class Sched:
    def __init__(self, nc, stack):
        self.nc = nc
        self.stack = stack
        self.engs = ["pe", "act", "dve", "pool", "sp"]
        self.stream = {e: [] for e in self.engs}
        self.cnt = {e: 0 for e in self.engs}
        self.pend = {e: False for e in self.engs}
        self.lastw = {}
        self.readers = {}
        self.waited = {e: {} for e in self.engs}
        self.sems = {}
        self.dcnt = {}
        self.bar = {}
        self.prefetch_sems = set()
        self.out_tokens = {}
        for e in ["pe", "act", "dve"]:
            self._sem(e)

    def _sem(self, name):
        if name not in self.sems:
            self.sems[name] = self.stack.enter_context(self.nc.semaphore("s_" + name))
        return self.sems[name]

    def _deps(self, eng, reads, writes, nobar):
        deps = {}

        def need(tok):
            if tok is None:
                return
            s, v = tok
            if deps.get(s, 0) < v:
                deps[s] = v

        for k in reads:
            need(self.lastw.get(k))
        for k in writes:
            need(self.lastw.get(k))
            for t in self.readers.get(k, ()):
                need(t)
        if not nobar:
            for s, v in self.bar.items():
                need((s, v))
        waits = []
        for s, v in deps.items():
            if s == eng and eng == "pe":
                continue
            if self.waited[eng].get(s, 0) < v:
                self.waited[eng][s] = v
                waits.append((s, v))
        return waits

    def _commit(self, tok, reads, writes):
        for k in reads:
            self.readers.setdefault(k, []).append(tok)
        for k in writes:
            self.lastw[k] = tok
            self.readers[k] = []

    def add(self, eng, fn, reads=(), writes=(), sig=True, nobar=False):
        waits = self._deps(eng, reads, writes, nobar)
        if sig:
            self.cnt[eng] += 1
            tok = (eng, self.cnt[eng])
            self.pend[eng] = False
        else:
            tok = (eng, self.cnt[eng] + 1)
            self.pend[eng] = True
        self.stream[eng].append((waits, fn, eng if sig else None, 1))
        self._commit(tok, reads, writes)
        return tok

    def dma(self, q, out, in_, sem, reads=(), writes=(), nobar=False, is_out=False):
        waits = self._deps(q, reads, writes, nobar)
        self._sem(sem)
        self.dcnt[sem] = self.dcnt.get(sem, 0) + 1
        tok = (sem, 16 * self.dcnt[sem])
        self.stream[q].append((waits, lambda e: e.dma_start(out=out, in_=in_), sem, 16))
        self._commit(tok, reads, writes)
        if is_out:
            self.out_tokens[sem] = tok[1]
        return tok

    def retoken(self, keys, sem):
        tok = (sem, 16 * self.dcnt[sem])
        for k in keys:
            self.lastw[k] = tok

    def barrier(self):
        for e in ["pe", "act", "dve"]:
            assert not self.pend[e], e
            if self.cnt[e] > 0:
                self.bar[e] = self.cnt[e]
        for s, n in self.dcnt.items():
            if s not in self.prefetch_sems:
                self.bar[s] = 16 * n

    def finish(self):
        waits = []
        for s, v in self.out_tokens.items():
            waits.append((s, v))
        for e in ["pe", "act", "dve"]:
            waits.append((e, self.cnt[e]))
        self.stream["sp"].append((waits, None, None, 0))

    def emit(self, block):
        def run(name, eng):
            for waits, fn, sem, inc in self.stream[name]:
                for s, v in waits:
                    eng.wait_ge(self.sems[s], v)
                if fn is not None:
                    ins = fn(eng)
                    if sem is not None:
                        ins.then_inc(self.sems[sem], inc)

        @block.tensor
        def _(e):
            run("pe", e)

        @block.scalar
        def _(e):
            run("act", e)

        @block.vector
        def _(e):
            run("dve", e)

        @block.gpsimd
        def _(e):
            run("pool", e)

        @block.sync
        def _(e):
            run("sp", e)


class Bump:
    def __init__(self, arena, base, limit):
        self.arena, self.base, self.limit, self.p = arena, base, limit, base

    def reset(self, base=None, limit=None):
        if base is not None:
            self.base = base
        if limit is not None:
            self.limit = limit
        self.p = self.base

    def alloc(self, shape, dt, parts=128):
        n = int(np.prod(shape))
        nb = n * (4 if dt == F32 else 2)
        nb_al = (nb + 63) // 64 * 64
        off = self.p
        assert off + nb_al <= self.limit, (off, nb_al, self.limit)
        self.p += nb_al
        self.last = off
        return view(self.arena, off, shape, dt, parts)


def view(arena, off, shape, dt, parts=128):
    n = int(np.prod(shape))
    assert off % 4 == 0
    if dt == F32:
        v = arena[:, off // 2: off // 2 + 2 * n].bitcast(F32)
    else:
        v = arena[:, off // 2: off // 2 + n]
    if parts != 128:
        v = v[0:parts]
    if len(shape) == 2:
        v = v.rearrange("p (a b) -> p a b", a=shape[0])
    elif len(shape) == 3:
        v = v.rearrange("p (a b c) -> p a b c", a=shape[0], b=shape[1])
    elif len(shape) == 4:
        v = v.rearrange("p (a b c d) -> p a b c d", a=shape[0], b=shape[1], c=shape[2])
    return v


def bc(ap, shape):
    return ap.broadcast_to(list(shape))


def build_program():
    nc = bass.Bass("TRN2", target_bir_lowering=False)

    def din(name, shape):
        return nc.dram_tensor(name, list(shape), F32, kind="ExternalInput").ap()

    def dout(name, shape):
        return nc.dram_tensor(name, list(shape), F32, kind="ExternalOutput").ap()

    xm = din("xm", [T, D])
    xp = din("xp", [1024, D])
    cc = din("cc", [17, D])
    st = din("st", [16, 8, 128, 128])
    w_ada = din("w_ada", [D, 6 * D])
    b_ada = din("b_ada", [1, 6 * D])
    gvec = din("gvec", [4, D])
    w_in = din("w_in", [D, 6144])
    w_o = din("w_o", [D, D])
    w_ff1 = din("w_ff1", [D, 4 * D])
    w_ff2 = din("w_ff2", [4 * D, D])
    lngb = din("lngb", [2, 1024])
    wsT_d = din("wsT", [128, 2, 8, 128])
    bcol_d = din("bcol", [128, 2, 8])
    tabm_d = din("tabm", [128, NCH, 2, 64])
    tabp_d = din("tabp", [128, 8, 2, 64])
    tsc_d = din("tsc", [128, 6, 8])
    pdec_d = din("pdec", [128, 8, 8])
    caus_d = din("caus", [128, 2, 128])
    bmask_d = din("bmask", [128, 16, 128])
    tmask_d = din("tmask", [128, 16])
    ident_d = din("ident", [128, 128])

    y_o = dout("y", [T, D])
    sp_o = dout("sp_out", [8, 128, 128])
    ss_o = dout("ss_out", [16, 8, 128, 128])
    vn_o = dout("vn_out", [128, 1024])
    x1s = nc.dram_tensor("x1s", [T, D], F32, kind="Internal").ap()
    ggd = nc.dram_tensor("ggd", [2, 17, D], F32, kind="Internal").ap()

    stack = contextlib.ExitStack()
    with stack:
        arena = stack.enter_context(nc.sbuf_tensor("arena", [128, ARENA_B // 2], BF16))
        pb = [stack.enter_context(nc.psum_tensor("pb%d" % i, [128, 512], F32)) for i in range(7)]
        pb7 = stack.enter_context(nc.psum_tensor("pb7", [128, 1024], BF16))
        S = Sched(nc, stack)
        A = S.add

        ringv = [view(arena, OFF_RING + i * SLOT_B, [16, 512], BF16) for i in range(NSLOT)]
        ringv8 = [view(arena, OFF_RING + i * SLOT_B, [8, 512], BF16) for i in range(NSLOT)]
        actT = view(arena, OFF_ACT, [16, T], BF16)
        mixT = view(arena, OFF_BIG, [16, T], BF16)
        CB = Bump(arena, OFF_CONST, ARENA_B)
        ident_f = CB.alloc([128], F32)
        ident_b = CB.alloc([128], BF16)
        abT = [CB.alloc([16, 17], F32) for _ in range(4)]
        stat = CB.alloc([64], F32)
        epsc = stat[:, 32:33]
        OFF_P3TMP = CB.p
        scT = CB.alloc([16, 17], BF16)
        badap = CB.alloc([512], F32, parts=17)
        gbp = CB.alloc([512], F32, parts=17)
        secp = CB.alloc([512], F32, parts=17)
        tabm = CB.alloc([NCH, 2, 64], F32)
        tsc = CB.alloc([6, 8], F32)
        bcol = CB.alloc([2, 8], F32)
        tmask = CB.alloc([16], F32)
        caus = CB.alloc([2, 128], BF16)
        S_f = CB.alloc([8, 128], F32)
        Sg_b = CB.alloc([8, 128], BF16)
        OFF_MIXTMP = CB.p
        bmask = view(arena, OFF_BIG + 36864, [16, 128], BF16)

        panels = []
        wav = w_ada.rearrange("(k p) n -> p k n", p=128)
        wiv = w_in.rearrange("(k p) n -> p k n", p=128)
        wov = w_o.rearrange("(k p) n -> p k n", p=128)
        w1v = w_ff1.rearrange("(k p) n -> p k n", p=128)
        w2v = w_ff2.rearrange("(k p) n -> p k n", p=128)
        for i in range(8):
            panels.append((wav[:, :, i * 512:(i + 1) * 512], 16))
        for c0 in (1024, 1536, 2048, 2560):
            panels.append((wiv[:, :, c0:c0 + 512], 16))
        for gi_, grp in enumerate(((1024, 2048, 0, 3072), (1536, 2560, 512, 3584), (5120, 5632, 4096, 4608))):
            for c0 in grp:
                panels.append((wiv[:, :, c0:c0 + 512], 16))
            if gi_ < 2:
                for i in range(8 + gi_ * 8, 16 + gi_ * 8):
                    panels.append((wav[:, :, i * 512:(i + 1) * 512], 16))
        for nb in range(4):
            panels.append((wov[:, :, nb * 512:(nb + 1) * 512], 16))
        for half in range(2):
            for gk in range(8):
                for pj in range(2):
                    c0 = gk * 1024 + pj * 512
                    panels.append((w1v[:, :, c0:c0 + 512], 16))
                for nb in range(4):
                    panels.append((w2v[:, gk * 8:(gk + 1) * 8, nb * 512:(nb + 1) * 512], 8))

        ring = {"next_dma": 0, "next_acq": 0, "free": list(range(NSLOT)), "slot_of": {}}
        for i in range(NSLOT):
            S.prefetch_sems.add("ring%d" % i)

        def ring_pump():
            while ring["next_dma"] < len(panels) and ring["free"]:
                i = ring["next_dma"]
                slot = ring["free"].pop(0)
                ap, nk = panels[i]
                dst = ringv[slot] if nk == 16 else ringv8[slot]
                S.dma("pool", dst, ap, "ring%d" % slot, writes=[("ring", slot)], nobar=True)
                ring["slot_of"][i] = slot
                ring["next_dma"] += 1

        def acquire():
            i = ring["next_acq"]
            ring["next_acq"] += 1
            if i not in ring["slot_of"]:
                ring_pump()
            assert i in ring["slot_of"], "ring overflow"
            slot = ring["slot_of"][i]
            nk = panels[i][1]
            return (slot, ringv[slot] if nk == 16 else ringv8[slot], ("ring", slot))

        def release(h):
            ring["free"].append(h[0])
            ring_pump()

        ckeys = []

        def cload(dst, src, key):
            S.dma("sp", dst, src, "const", writes=[key])
            ckeys.append(key)

        cload(ident_f, ident_d, "ident_f")
        cload(tabm, tabm_d, "tabm")
        cload(tsc, tsc_d, "tsc")
        cload(bcol, bcol_d, "bcol")
        cload(tmask, tmask_d, "tmask")
        S.retoken(ckeys, "const")
        A("dve", lambda e: e.memset(epsc, EPS), writes=["epsc"])
        pkeys = []
        for dst, src, key in ((ident_b, ident_d, "ident_b"), (caus, caus_d, "caus")):
            S.dma("pool", dst, src, "constb", writes=[key])
            pkeys.append(key)
        S.retoken(pkeys, "constb")
        ring_pump()

        def rstd_from_ss(ss_ap, out_ap, n, rk, wk):
            A("act", lambda e: e.activation(out=out_ap, in_=ss_ap, func=AF.Sqrt, scale=1.0 / n, bias=epsc),
              reads=rk + ["epsc"], writes=[wk])
            A("dve", lambda e: e.reciprocal(out=out_ap, in_=out_ap), reads=[wk], writes=[wk])

        def norm_rows(xin, xin_key, xn, xn_key, ss, ss_key, rs, rs_key):
            A("act", lambda e: e.activation(out=xn, in_=xin, func=AF.Square, accum_out=ss),
              reads=[xin_key, ss_key], writes=[xn_key, ss_key])
            rstd_from_ss(ss, rs, float(D), [ss_key], rs_key)
            A("act", lambda e: e.activation(out=xn, in_=xin, func=AF.Copy, scale=rs),
              reads=[xin_key, rs_key], writes=[xn_key])

        def transpose_mod(xn, xn_key, dst, dst_keys, aT, bT, ty, bank0=5):
            for q4 in range(4):
                bk = bank0 + (q4 % 2)
                for kk in range(4):
                    k = q4 * 4 + kk
                    A("pe", lambda e, k=k, kk=kk, bk=bk: e.transpose(
                        pb[bk][:, kk * 128:(kk + 1) * 128], xn[:, k * 128:(k + 1) * 128], ident_f),
                      reads=[xn_key, "ident_f"], writes=[("pb", bk)])
                pv = pb[bk][:, :].rearrange("p (a b) -> p a b", a=4)
                dv = dst[:, q4 * 4:(q4 + 1) * 4, :]
                if ty == 0 and q4 % 2 == 0:
                    for kk in range(4):
                        k = q4 * 4 + kk
                        A("act", lambda e, k=k, kk=kk, bk=bk: e.activation(
                            out=dst[:, k, :], in_=pb[bk][:, kk * 128:(kk + 1) * 128], func=AF.Identity,
                            scale=aT[:, k, 0:1], bias=bT[:, k, 0:1]),
                          reads=[("pb", bk), "abT"], writes=dst_keys)
                elif ty == 0:
                    a_b = bc(aT[:, q4 * 4:(q4 + 1) * 4, 0:1], [128, 4, 128])
                    b_b = bc(bT[:, q4 * 4:(q4 + 1) * 4, 0:1], [128, 4, 128])
                    A("dve", lambda e, pv=pv, a_b=a_b: e.tensor_tensor(out=pv, in0=pv, in1=a_b, op=ALU.mult),
                      reads=[("pb", bk), "abT"], writes=[("pb", bk)])
                    A("dve", lambda e, pv=pv, b_b=b_b, dv=dv: e.tensor_tensor(out=dv, in0=pv, in1=b_b, op=ALU.add),
                      reads=[("pb", bk), "abT"], writes=dst_keys)
                else:
                    for kk in range(4):
                        k = q4 * 4 + kk
                        pv1 = pb[bk][:, kk * 128:(kk + 1) * 128].rearrange("p (s j) -> p s j", s=16)
                        dv1 = dst[:, k, :].rearrange("p (s j) -> p s j", s=16)
                        a_b = bc(aT[:, k, 1:17].unsqueeze(2), [128, 16, 8])
                        b_b = bc(bT[:, k, 1:17].unsqueeze(2), [128, 16, 8])
                        A("dve", lambda e, pv1=pv1, a_b=a_b: e.tensor_tensor(out=pv1, in0=pv1, in1=a_b, op=ALU.mult),
                          reads=[("pb", bk), "abT"], writes=[("pb", bk)])
                        A("dve", lambda e, pv1=pv1, b_b=b_b, dv1=dv1: e.tensor_tensor(out=dv1, in0=pv1, in1=b_b, op=ALU.add),
                          reads=[("pb", bk), "abT"], writes=dst_keys)

        def proj(bank, lhs_fn, h, lhs_keys, n=512):
            for k in range(16):
                A("pe", lambda e, k=k: e.matmul(pb[bank][:, 0:n], lhsT=lhs_fn(k), rhs=h[1][:, k, 0:n],
                                                start=(k == 0), stop=(k == 15)),
                  reads=lhs_keys + [h[2]], writes=[("pb", bank)], sig=(k == 15))

        def rotary(bank, tab, c, out4, out_key, tA, tB):
            pv = pb[bank][:, :].rearrange("p (h t d) -> p h t d", h=4, t=2)
            x1, x2 = pv[:, :, 0, :], pv[:, :, 1, :]
            cos = bc(tab[:, c, 0:1, :], [128, 4, 64])
            sin = bc(tab[:, c, 1:2, :], [128, 4, 64])
            rk = [("pb", bank), "tab"]
            A("dve", lambda e: e.tensor_tensor(out=tA, in0=x1, in1=cos, op=ALU.mult), reads=rk, writes=["rtA"])
            A("dve", lambda e: e.tensor_tensor(out=tB, in0=x2, in1=sin, op=ALU.mult), reads=rk, writes=["rtB"])
            A("dve", lambda e: e.tensor_tensor(out=out4[:, :, 0, :], in0=tA, in1=tB, op=ALU.subtract),
              reads=["rtA", "rtB"], writes=[out_key])
            A("dve", lambda e: e.tensor_tensor(out=tA, in0=x1, in1=sin, op=ALU.mult), reads=rk, writes=["rtA"])
            A("dve", lambda e: e.tensor_tensor(out=tB, in0=x2, in1=cos, op=ALU.mult), reads=rk, writes=["rtB"])
            A("dve", lambda e: e.tensor_tensor(out=out4[:, :, 1, :], in0=tA, in1=tB, op=ALU.add),
              reads=["rtA", "rtB"], writes=[out_key])

        TB = Bump(arena, OFF_BIG, OFF_BIG + BIG_B)
        cc_sb = TB.alloc([D], F32, parts=17)
        hTp_all = TB.alloc([16, 1024], BF16)
        S.dma("sp", cc_sb, cc, "misc0", writes=["cc_sb"])
        A("act", lambda e: e.activation(out=cc_sb, in_=cc_sb, func=AF.Silu), reads=["cc_sb"], writes=["cc_sb"])
        for k in range(16):
            A("pe", lambda e, k=k: e.transpose(pb[5][:, k * 17:(k + 1) * 17], cc_sb[:, k * 128:(k + 1) * 128],
                                               ident_f[0:17, 0:17]),
              reads=["cc_sb", "ident_f"], writes=[("pb", 5)])
        A("act", lambda e: e.activation(out=scT, in_=pb[5][:, 0:272].rearrange("p (a b) -> p a b", a=16), func=AF.Copy),
          reads=[("pb", 5)], writes=["scT"])

        def mod_a(i, mbank=4):
            s_, nbk = i // 4, i % 4
            cs = slice(s_ * D + nbk * 512, s_ * D + (nbk + 1) * 512)
            S.dma("sp", badap, bc(b_ada[0:1, cs], [17, 512]), "misc1", writes=["badap"])
            if s_ in (1, 2, 4, 5):
                gi = {1: 0, 2: 1, 4: 2, 5: 3}[s_]
                S.dma("sp", gbp, bc(gvec[gi:gi + 1, nbk * 512:(nbk + 1) * 512], [17, 512]), "misc2", writes=["gbp"])
            h = acquire()
            for k in range(16):
                A("pe", lambda e, k=k, h=h: e.matmul(pb[mbank][0:17, :], lhsT=scT[:, k, :], rhs=h[1][:, k, :],
                                                     start=(k == 0), stop=(k == 15)),
                  reads=["scT", h[2]], writes=[("pb", mbank)], sig=(k == 15))
            release(h)
            A("dve", lambda e: e.tensor_tensor(out=secp, in0=pb[mbank][0:17, :], in1=badap, op=ALU.add),
              reads=[("pb", mbank), "badap"], writes=["secp"])
            if s_ in (1, 4):
                A("dve", lambda e: e.scalar_tensor_tensor(out=secp, in0=secp, scalar=1.0, in1=gbp,
                                                          op0=ALU.add, op1=ALU.mult),
                  reads=["secp", "gbp"], writes=["secp"])
            if s_ in (2, 5):
                A("dve", lambda e: e.tensor_tensor(out=secp, in0=secp, in1=gbp, op=ALU.mult),
                  reads=["secp", "gbp"], writes=["secp"])

        def mod_b(i, tbank=6):
            s_, nbk = i // 4, i % 4
            if s_ in (2, 5):
                S.dma("sp", ggd[0 if s_ == 2 else 1][:, nbk * 512:(nbk + 1) * 512], secp, "misc3_%d" % s_,
                      reads=["secp"], writes=[("ggd", s_)])
            else:
                dstT = abT[{0: 0, 1: 1, 3: 2, 4: 3}[s_]]
                for kk in range(4):
                    A("pe", lambda e, kk=kk: e.transpose(pb[tbank][:, kk * 17:(kk + 1) * 17],
                                                         secp[:, kk * 128:(kk + 1) * 128], ident_f[0:17, 0:17]),
                      reads=["secp", "ident_f"], writes=[("pb", tbank)])
                A("act", lambda e, dstT=dstT, nbk=nbk: e.activation(
                    out=dstT[:, nbk * 4:(nbk + 1) * 4, :],
                    in_=pb[tbank][:, 0:68].rearrange("p (a b) -> p a b", a=4), func=AF.Copy),
                  reads=[("pb", tbank)], writes=["abT"])

        def mod_panel(i):
            mod_a(i)
            mod_b(i)

        HB = Bump(arena, OFF_MIXTMP, ARENA_B)
        xinH = [HB.alloc([D], F32) for _ in range(2)]
        xnH = [HB.alloc([D], BF16) for _ in range(2)]
        ss = stat[:, 0:1]
        rs = stat[:, 1:2]
        ssH = [stat[:, 2:3], stat[:, 3:4]]
        rsH = [stat[:, 4:5], stat[:, 5:6]]
        ssD = [stat[:, 6:7], stat[:, 7:8]]

        def Ha(i):
            j = i % 2
            xb, xkey = xinH[j], ("xinH", j)
            xn_, nkey = xnH[j], ("xnH", j)
            src = xp[i * 128:(i + 1) * 128, :] if i < 8 else xm[(i - 8) * 128:(i - 7) * 128, :]
            S.dma("sp", xb, src, "xinH%d" % j, writes=[xkey])
            A("act", lambda e: e.activation(out=xn_[:, 0:1024], in_=xb[:, 0:1024], func=AF.Square, accum_out=ssH[j]),
              reads=[xkey], writes=[nkey, ("ssH", j)])
            A("dve", lambda e: e.memset(ssD[j], 0.0), writes=[("ssD", j)])
            A("dve", lambda e: e.scalar_tensor_tensor(out=xn_[:, 1024:2048], in0=xb[:, 1024:2048], scalar=1.0,
                                                      in1=xb[:, 1024:2048], op0=ALU.mult, op1=ALU.mult, accum_out=ssD[j]),
              reads=[xkey, ("ssD", j)], writes=[(nkey, 1), ("ssD", j)])
            A("dve", lambda e: e.tensor_tensor(out=ssH[j], in0=ssH[j], in1=ssD[j], op=ALU.add),
              reads=[("ssH", j), ("ssD", j)], writes=[("ssH", j)])
            rstd_from_ss(ssH[j], rsH[j], float(D), [("ssH", j)], ("rsH", j))
            A("act", lambda e: e.activation(out=xn_[:, 0:1024], in_=xb[:, 0:1024], func=AF.Copy, scale=rsH[j]),
              reads=[xkey, ("rsH", j)], writes=[nkey])
            A("dve", lambda e: e.tensor_scalar(out=xn_[:, 1024:2048], in0=xb[:, 1024:2048], scalar1=rsH[j], scalar2=None,
                                               op0=ALU.mult),
              reads=[xkey, ("rsH", j)], writes=[(nkey, 1)])

        def Hb(i):
            j = i % 2
            xn_, nkey = xnH[j], ("xnH", j)
            for q8 in range(2):
                for kk in range(8):
                    k = q8 * 8 + kk
                    A("pe", lambda e, k=k, kk=kk: e.transpose(
                        pb7[:, kk * 128:(kk + 1) * 128], xn_[:, k * 128:(k + 1) * 128], ident_b),
                      reads=[nkey, (nkey, 1), "ident_b"], writes=["pb7"])
                pv = pb7[:, :].rearrange("p (a b) -> p a b", a=8)
                if i < 8:
                    dv, dkey = hTp_all[:, q8 * 8:(q8 + 1) * 8, i * 128:(i + 1) * 128], "hTp_all"
                else:
                    dv, dkey = actT[:, q8 * 8:(q8 + 1) * 8, (i - 8) * 128:(i - 7) * 128], ("actT", i - 8)
                A("dve", lambda e, pv=pv, dv=dv: e.tensor_copy(out=dv, in_=pv), reads=["pb7"], writes=[dkey])

        mi = 0
        Ha(0)
        for i in range(17):
            if i + 1 < 17:
                Ha(i + 1)
            Hb(i)
            if i % 2 == 1 and mi < 8:
                mod_panel(mi)
                mi += 1
        while mi < 8:
            mod_panel(mi)
            mi += 1
        S.barrier()
        for k in range(16):
            if k % 2 == 0:
                A("act", lambda e, k=k: e.activation(out=hTp_all[:, k, :], in_=hTp_all[:, k, :], func=AF.Identity,
                                                     scale=abT[1][:, k, 0:1], bias=abT[0][:, k, 0:1]),
                  reads=["hTp_all", "abT"], writes=["hTp_all"])
            else:
                A("dve", lambda e, k=k: e.tensor_scalar(out=hTp_all[:, k, :], in0=hTp_all[:, k, :],
                                                        scalar1=abT[1][:, k, 0:1], scalar2=abT[0][:, k, 0:1],
                                                        op0=ALU.mult, op1=ALU.add),
                  reads=["hTp_all", "abT"], writes=["hTp_all"])

        def modulate_main():
            for k in range(16):
                if k % 2 == 1:
                    A("act", lambda e, k=k: e.activation(out=actT[:, k, 0:1024], in_=actT[:, k, 0:1024], func=AF.Identity,
                                                         scale=abT[1][:, k, 0:1], bias=abT[0][:, k, 0:1]),
                      reads=[("actT", c_) for c_ in range(8)] + ["abT"], writes=[("actT", c_) for c_ in range(8)])
                else:
                    A("dve", lambda e, k=k: e.tensor_scalar(out=actT[:, k, 0:1024], in0=actT[:, k, 0:1024],
                                                            scalar1=abT[1][:, k, 0:1], scalar2=abT[0][:, k, 0:1],
                                                            op0=ALU.mult, op1=ALU.add),
                      reads=[("actT", c_) for c_ in range(8)] + ["abT"], writes=[("actT", c_) for c_ in range(8)])
                sv = actT[:, k, 1024:1152].rearrange("p (s j) -> p s j", s=16)
                A("dve", lambda e, k=k, sv=sv: e.tensor_tensor(out=sv, in0=sv, in1=bc(abT[1][:, k, 1:17].unsqueeze(2), [128, 16, 8]),
                                                               op=ALU.mult),
                  reads=[("actT", 8), "abT"], writes=[("actT", 8)])
                A("dve", lambda e, k=k, sv=sv: e.tensor_tensor(out=sv, in0=sv, in1=bc(abT[0][:, k, 1:17].unsqueeze(2), [128, 16, 8]),
                                                               op=ALU.add),
                  reads=[("actT", 8), "abT"], writes=[("actT", 8)])

        HB.reset()
        tabp = HB.alloc([8, 2, 64], F32)
        pdec = HB.alloc([8, 8], F32)
        krot1 = HB.alloc([4, 2, 64], F32)
        rtA1 = HB.alloc([4, 64], F32)
        rtB1 = HB.alloc([4, 64], F32)
        kdp = [[HB.alloc([512], BF16) for _ in range(2)] for _ in range(2)]
        vtp = [[HB.alloc([512], BF16) for _ in range(2)] for _ in range(2)]
        zer = HB.alloc([512], BF16)
        S.dma("sp", tabp, tabp_d, "misc0", writes=["tab"])
        S.dma("sp", pdec, pdec_d, "misc1", writes=["pdec"])
        hp = [acquire() for _ in range(4)]
        A("dve", lambda e: e.memset(zer, 0.0), writes=["zer"])
        for g4 in range(2):
            A("pe", lambda e, g4=g4: e.matmul(pb[4 + g4][:, :], lhsT=zer[:, 0:128], rhs=zer, start=True, stop=False),
              reads=["zer"], writes=[("pb", 4 + g4)])

        def prevP(pc, i):
            proj(i, lambda k: hTp_all[:, k, pc * 128:(pc + 1) * 128], hp[i], ["hTp_all"])
            pb_ = pc % 2
            if i < 2:
                rotary(i, tabp, pc, krot1, "krot", rtA1, rtB1)
                A("dve", lambda e: e.tensor_tensor(
                    out=kdp[pb_][i][:, :].rearrange("p (h d) -> p h d", h=4),
                    in0=krot1[:, :, :, :].rearrange("p h t d -> p h (t d)"),
                    in1=bc(pdec[:, pc, i * 4:(i + 1) * 4].unsqueeze(2), [128, 4, 128]), op=ALU.mult),
                  reads=["krot", "pdec"], writes=[("kdp", pb_, i)])
            else:
                A("act", lambda e: e.activation(out=vtp[pb_][i - 2], in_=pb[i][:, :], func=AF.Copy),
                  reads=[("pb", i)], writes=[("vtp", pb_, i - 2)])

        def prevS(pc):
            pb_ = pc % 2
            for hd in range(8):
                g4, hh = hd // 4, hd % 4
                A("pe", lambda e, g4=g4, hh=hh: e.matmul(
                    pb[4 + g4][:, hh * 128:(hh + 1) * 128], lhsT=kdp[pb_][g4][:, hh * 128:(hh + 1) * 128],
                    rhs=vtp[pb_][g4][:, hh * 128:(hh + 1) * 128], start=False, stop=(pc == 7 and hh == 3)),
                  reads=[("kdp", pb_, g4), ("vtp", pb_, g4)], writes=[("pb", 4 + g4)])

        for i in range(4):
            prevP(0, i)
        modulate_main()
        for pc in range(8):
            if pc + 1 < 8:
                prevP(pc + 1, 0)
            prevS(pc)
            if pc + 1 < 8:
                for i in range(1, 4):
                    prevP(pc + 1, i)
        for i in range(4):
            release(hp[i])
        for g4 in range(2):
            A("dve", lambda e, g4=g4: e.tensor_copy(
                out=S_f[:, g4 * 4:(g4 + 1) * 4, :], in_=pb[4 + g4][:, :].rearrange("p (h d) -> p h d", h=4)),
              reads=[("pb", 4 + g4)], writes=[("S_f", g4 * 4 + i_) for i_ in range(4)])
            for hh in range(4):
                hd = g4 * 4 + hh
                A("act", lambda e, hd=hd: e.activation(out=Sg_b[:, hd, :], in_=S_f[:, hd, :], func=AF.Copy,
                                                       scale=float(GAM[hd])),
                  reads=[("S_f", hd)], writes=[("Sg_b", hd)])
        S.barrier()
        S.dma("pool", bmask, bmask_d, "bmaskl", writes=["bmask"])

        MB = Bump(arena, OFF_MIXTMP, ARENA_B)
        krot = MB.alloc([4, 2, 64], F32)
        off_krot = MB.last
        qrot = MB.alloc([4, 2, 64], F32)
        off_qrot = MB.last
        rtA = MB.alloc([4, 64], F32)
        off_rtA = MB.last
        rtB = MB.alloc([4, 64], F32)
        ktil, kdec, qtil, vtok, gs = [], [], [], [], []
        for _ in range(2):
            for lst in (ktil, kdec, qtil, vtok, gs):
                lst.append(MB.alloc([512], BF16))
            if _ == 0:
                off_dbl = MB.p
        kqT = MB.alloc([1024], BF16)
        sMT = MB.alloc([512], BF16)
        o_sb = MB.alloc([512], F32)
        ro = MB.alloc([512], BF16)
        bst = MB.alloc([4, 6], F32)
        mv = MB.alloc([4, 2], F32)
        rs4 = MB.alloc([4], F32)
        nmr = MB.alloc([4], F32)
        Sin = [MB.alloc([4, 128], F32), view(arena, off_krot, [4, 128], F32)]
        Sgs = [MB.alloc([4, 128], BF16), view(arena, off_qrot, [4, 128], BF16)]
        qm = [MB.alloc([4, 128], BF16), view(arena, off_qrot + 1024, [4, 128], BF16)]
        vm = [MB.alloc([4, 128], BF16), view(arena, off_rtA, [4, 128], BF16)]
        Snew = [view(arena, off_dbl, [4, 128], F32), view(arena, off_dbl + 2048, [4, 128], F32)]

        def lhs_act(c):
            return lambda k: actT[:, k, c * 128:(c + 1) * 128]

        def v4(ap):
            return ap[:, :].rearrange("p (h d) -> p h d", h=4)

        def mix_group(G):
            hk, hv, hq, hg = acquire(), acquire(), acquire(), acquire()
            hs = slice(G * 4, G * 4 + 4)
            reg_off = OFF_BIG + (G + 1) * 4 * T * 2
            nbuf = ((16 - (G + 1) * 4) * T * 2) // 2048
            nox = 6 if G == 0 else 4
            npre = nbuf - nox
            preS = [view(arena, reg_off + i_ * 2048, [4, 128], F32) for i_ in range(npre)]
            SnewX = [view(arena, reg_off + (npre + i_) * 2048, [4, 128], F32) for i_ in range(nox)]
            for t_ in range(npre):
                hh_, s2_ = divmod(t_, 4)
                S.dma("sp", preS[t_], st[s2_ * 4:s2_ * 4 + 4, G * 4 + hh_].rearrange("s d e -> d s e"),
                      "sinP%d" % G, writes=(["mixS_all"] if t_ == 0 else []) + [("mixS", t_)])
            S.retoken([("mixS", t_) for t_ in range(npre)] + ["mixS_all"], "sinP%d" % G)

            def tyof(c):
                return 0 if c < 8 else 1

            def P_k(c):
                proj(0, lhs_act(c), hk, [("actT", c)])

            def P_v(c):
                proj(1, lhs_act(c), hv, [("actT", c)])

            def P_q(c):
                proj(2, lhs_act(c), hq, [("actT", c)])

            def P_g(c):
                proj(3, lhs_act(c), hg, [("actT", c)])

            def E_k(c):
                ty, b = tyof(c), c % 2
                rotary(0, tabm, c, krot, "krot", rtA, rtB)
                kr3 = krot[:, :, :, :].rearrange("p h t d -> p h (t d)")
                A("dve", lambda e: e.tensor_tensor(out=v4(ktil[b]), in0=kr3,
                                                   in1=bc(tsc[:, 3 * ty + 1, hs].unsqueeze(2), [128, 4, 128]), op=ALU.mult),
                  reads=["krot", "tsc"], writes=[("ktil", b)])
                A("dve", lambda e: e.tensor_tensor(out=v4(kdec[b]), in0=kr3,
                                                   in1=bc(tsc[:, 3 * ty + 2, hs].unsqueeze(2), [128, 4, 128]), op=ALU.mult),
                  reads=["krot", "tsc"], writes=[("kdec", b)])

            def E_v(c):
                b = c % 2
                A("act", lambda e: e.activation(out=vtok[b], in_=pb[1][:, :], func=AF.Copy),
                  reads=[("pb", 1)], writes=[("vtok", b)])

            def E_q(c):
                ty, b = tyof(c), c % 2
                rotary(2, tabm, c, qrot, "qrot", rtA, rtB)
                A("dve", lambda e: e.tensor_tensor(out=v4(qtil[b]),
                                                   in0=qrot[:, :, :, :].rearrange("p h t d -> p h (t d)"),
                                                   in1=bc(tsc[:, 3 * ty + 0, hs].unsqueeze(2), [128, 4, 128]), op=ALU.mult),
                  reads=["qrot", "tsc"], writes=[("qtil", b)])

            def E_g(c):
                b = c % 2
                A("act", lambda e: e.activation(out=gs[b], in_=pb[3][:, :], func=AF.Silu),
                  reads=[("pb", 3)], writes=[("gs", b)])

            def C1(c):
                b = c % 2
                for hh in range(4):
                    A("pe", lambda e, hh=hh: e.transpose(pb7[:, hh * 128:(hh + 1) * 128],
                                                         ktil[b][:, hh * 128:(hh + 1) * 128], ident_b),
                      reads=[("ktil", b), "ident_b"], writes=["pb7"])
                for hh in range(4):
                    A("pe", lambda e, hh=hh: e.transpose(pb7[:, 512 + hh * 128:512 + (hh + 1) * 128],
                                                         qtil[b][:, hh * 128:(hh + 1) * 128], ident_b),
                      reads=[("qtil", b), "ident_b"], writes=["pb7"])
                A("act", lambda e: e.activation(out=kqT, in_=pb7[:, :], func=AF.Copy), reads=["pb7"], writes=["kqT"])

            def C2(c):
                ty = tyof(c)
                for hh in range(4):
                    A("pe", lambda e, hh=hh: e.matmul(pb[4][:, hh * 128:(hh + 1) * 128],
                                                      lhsT=kqT[:, hh * 128:(hh + 1) * 128],
                                                      rhs=kqT[:, 512 + hh * 128:512 + (hh + 1) * 128],
                                                      start=True, stop=True),
                      reads=["kqT"], writes=[("pb", 4)])
                A("dve", lambda e: e.tensor_tensor(
                    out=v4(sMT), in0=v4(pb[4]), in1=bc(caus[:, ty:ty + 1, :], [128, 4, 128]), op=ALU.mult),
                  reads=[("pb", 4), "caus"], writes=["sMT"])

            def C3_prompt(c):
                b = c % 2
                for hh in range(4):
                    hd = G * 4 + hh
                    A("pe", lambda e, hh=hh: e.matmul(pb[5][:, hh * 128:(hh + 1) * 128],
                                                      lhsT=sMT[:, hh * 128:(hh + 1) * 128],
                                                      rhs=vtok[b][:, hh * 128:(hh + 1) * 128], start=True, stop=False),
                      reads=["sMT", ("vtok", b)], writes=[("pb", 5)], sig=False)
                    A("pe", lambda e, hh=hh, hd=hd: e.matmul(pb[5][:, hh * 128:(hh + 1) * 128],
                                                             lhsT=kqT[:, 512 + hh * 128:512 + (hh + 1) * 128],
                                                             rhs=Sg_b[:, hd, :], start=False, stop=True),
                      reads=["kqT", ("Sg_b", hd)], writes=[("pb", 5)])
                for hh in range(4):
                    A("pe", lambda e, hh=hh: e.matmul(pb[6][:, hh * 128:(hh + 1) * 128],
                                                      lhsT=kdec[b][:, hh * 128:(hh + 1) * 128],
                                                      rhs=vtok[b][:, hh * 128:(hh + 1) * 128], start=True, stop=True),
                      reads=[("kdec", b), ("vtok", b)], writes=[("pb", 6)])

            def C3_state(c):
                for hh in range(4):
                    hd = G * 4 + hh
                    A("dve", lambda e, hh=hh, hd=hd: e.scalar_tensor_tensor(
                        out=S_f[:, hd, :], in0=S_f[:, hd, :], scalar=float(GAM[hd] ** 128),
                        in1=pb[6][:, hh * 128:(hh + 1) * 128], op0=ALU.mult, op1=ALU.add),
                      reads=[("S_f", hd), ("pb", 6)], writes=[("S_f", hd)])
                    A("act", lambda e, hd=hd: e.activation(out=Sg_b[:, hd, :], in_=S_f[:, hd, :], func=AF.Copy,
                                                           scale=float(GAM[hd])),
                      reads=[("S_f", hd)], writes=[("Sg_b", hd)])
                if c == 7:
                    S.dma("sp", sp_o[G * 4:(G + 1) * 4].rearrange("h d e -> d h e"), S_f[:, G * 4:(G + 1) * 4, :],
                          "spo%d" % G, reads=[("S_f", G * 4 + i_) for i_ in range(4)], writes=[("spo", G)], is_out=True)

            def C3_sample(c):
                b = c % 2
                SinB = [Sin[0], Sin[1], o_sb[:, :].rearrange("p (s e) -> p s e", s=4), S_f[:, G * 4:(G + 1) * 4, :]]
                SnewB = [Snew[0], Snew[1]] + SnewX
                nsn = len(SnewB)
                fence_keys = ["krot", "qrot", "rtA", ("Sin", 1), ("Sin", 2), ("Sin", 3), ("Sgs", 1), ("qm", 1), ("vm", 1),
                              ("ktil", 1), ("kdec", 1), ("qtil", 1), ("vtok", 1), ("Snew", 0), ("Snew", 1), "o_sb"] + \
                             [("S_f", G * 4 + i_) for i_ in range(4)]
                A("dve", lambda e: e.memset(stat[:, 40:41], 0.0), writes=fence_keys)

                def src_of(t):
                    if t < npre:
                        return preS[t], ("mixS", t)
                    return SinB[t % 4], ("Sin", t % 4)

                def load_in(t):
                    if t < npre or t >= 16:
                        return
                    hh_, s2_ = divmod(t, 4)
                    j_ = t % 4
                    S.dma("sp", SinB[j_], st[s2_ * 4:s2_ * 4 + 4, G * 4 + hh_].rearrange("s d e -> d s e"),
                          "sin%d" % j_, writes=[("Sin", j_)])

                for t0 in range(npre, min(16, npre + 3)):
                    load_in(t0)
                for t in range(16):
                    hh, s2 = divmod(t, 4)
                    hd = G * 4 + hh
                    j = t % 2
                    jn = t % nsn
                    sin_ap, sin_key = src_of(t)
                    sq = slice(s2 * 4, s2 * 4 + 4)
                    if s2 == 0:
                        A("pe", lambda e, hh=hh: e.matmul(pb[5][:, hh * 128:(hh + 1) * 128],
                                                          lhsT=sMT[:, hh * 128:(hh + 1) * 128],
                                                          rhs=vtok[b][:, hh * 128:(hh + 1) * 128], start=True, stop=False),
                          reads=["sMT", ("vtok", b)], writes=[("pb", 5)], sig=False)
                    if t >= npre:
                        load_in(t + 3)
                    A("act", lambda e, hd=hd, j=j, sin_ap=sin_ap: e.activation(out=Sgs[j], in_=sin_ap, func=AF.Copy,
                                                                               scale=float(GAM[hd])),
                      reads=[sin_key, "mixS_all"], writes=[("Sgs", j)])
                    A("dve", lambda e, hh=hh, sq=sq, j=j: e.tensor_tensor(
                        out=qm[j], in0=bc(kqT[:, 512 + hh * 128:512 + (hh + 1) * 128].unsqueeze(1), [128, 4, 128]),
                        in1=bmask[:, sq, :], op=ALU.mult),
                      reads=["kqT", "bmask"], writes=[("qm", j)])
                    A("dve", lambda e, hh=hh, sq=sq, j=j: e.tensor_tensor(
                        out=vm[j], in0=bc(vtok[b][:, hh * 128:(hh + 1) * 128].unsqueeze(1), [128, 4, 128]),
                        in1=bc(tmask[:, sq].unsqueeze(2), [128, 4, 128]), op=ALU.mult),
                      reads=[("vtok", b), "tmask"], writes=[("vm", j)])
                    for sj in range(4):
                        last = (s2 == 3 and sj == 3)
                        A("pe", lambda e, hh=hh, sj=sj, last=last, j=j: e.matmul(
                            pb[5][:, hh * 128:(hh + 1) * 128], lhsT=qm[j][:, sj, :], rhs=Sgs[j][:, sj, :],
                            start=False, stop=last),
                          reads=[("qm", j), ("Sgs", j)], writes=[("pb", 5)], sig=last)
                    bk = 6 if j == 0 else 4
                    A("pe", lambda e, hh=hh, bk=bk, j=j: e.matmul(
                        pb[bk][:, :], lhsT=kdec[b][:, hh * 128:(hh + 1) * 128],
                        rhs=vm[j][:, :, :].rearrange("p s e -> p (s e)"), start=True, stop=True),
                      reads=[("kdec", b), ("vm", j)], writes=[("pb", bk)])
                    A("dve", lambda e, bk=bk, hd=hd, jn=jn, sin_ap=sin_ap: e.scalar_tensor_tensor(
                        out=SnewB[jn], in0=sin_ap, scalar=float(GAM[hd] ** 8),
                        in1=pb[bk][:, :].rearrange("p (s e) -> p s e", s=4), op0=ALU.mult, op1=ALU.add),
                      reads=[sin_key, ("pb", bk), "mixS_all"], writes=[("Snew", jn)])
                    S.dma("sp", ss_o[sq, hd].rearrange("s d e -> d s e"), SnewB[jn], "sso%d" % jn,
                          reads=[("Snew", jn), "mixS_all"], writes=[("sso", hd, s2)], is_out=True)
                A("dve", lambda e: e.memset(stat[:, 40:41], 0.0), writes=fence_keys)

            def C3_norm(c):
                b = c % 2
                for hh in range(4):
                    A("dve", lambda e, hh=hh: e.bn_stats(out=bst[:, hh, :], in_=pb[5][:, hh * 128:(hh + 1) * 128]),
                      reads=[("pb", 5)], writes=["bst"])
                for hh in range(4):
                    A("dve", lambda e, hh=hh: e.bn_aggr(out=mv[:, hh, :], in_=bst[:, hh, :]), reads=["bst"], writes=["mv"])
                rstd_from_ss(mv[:, :, 1], rs4, 1.0, ["mv"], "rs4")
                A("dve", lambda e: e.scalar_tensor_tensor(out=nmr, in0=mv[:, :, 0], scalar=-1.0, in1=rs4,
                                                          op0=ALU.mult, op1=ALU.mult),
                  reads=["mv", "rs4"], writes=["nmr"])
                for hh in range(4):
                    A("act", lambda e, hh=hh: e.activation(
                        out=o_sb[:, hh * 128:(hh + 1) * 128], in_=pb[5][:, hh * 128:(hh + 1) * 128], func=AF.Identity,
                        scale=rs4[:, hh:hh + 1], bias=nmr[:, hh:hh + 1]),
                      reads=[("pb", 5), "rs4", "nmr"], writes=["o_sb"])
                A("dve", lambda e: e.tensor_tensor(out=ro, in0=o_sb, in1=gs[b], op=ALU.mult),
                  reads=["o_sb", ("gs", b)], writes=["ro"])

            def C4(c):
                for hh in range(4):
                    A("pe", lambda e, hh=hh: e.transpose(pb7[:, hh * 128:(hh + 1) * 128],
                                                         ro[:, hh * 128:(hh + 1) * 128], ident_b),
                      reads=["ro", "ident_b"], writes=["pb7"])
                A("act", lambda e: e.activation(
                    out=mixT[:, G * 4:(G + 1) * 4, c * 128:(c + 1) * 128], in_=v4(pb7[:, 0:512]), func=AF.Copy),
                  reads=["pb7"], writes=[("mixT", c)] + (["mixS_all"] if G == 1 else []))

            P_k(0); E_k(0); P_v(0); E_v(0); P_q(0); E_q(0); P_g(0); E_g(0)
            for c in range(8):
                n = c + 1
                P_k(n)
                if n == 8:
                    release(hk)
                if c > 0:
                    mod_b(8 + G * 8 + c - 1, tbank=4)
                    C4(c - 1)
                C1(c)
                if c > 0:
                    E_g(c)
                E_k(n)
                P_v(n)
                if n == 8:
                    release(hv)
                E_v(n)
                C2(c)
                P_q(n)
                if n == 8:
                    release(hq)
                E_q(n)
                C3_prompt(c)
                C3_norm(c)
                C3_state(c)
                mod_a(8 + G * 8 + c)
                P_g(n)
                if n == 8:
                    release(hg)
            mod_b(8 + G * 8 + 7, tbank=4)
            C4(7)
            C1(8)
            E_g(8)
            C2(8)
            C3_sample(8)
            C3_norm(8)
            C4(8)

        for G in range(2):
            mix_group(G)

        S.barrier()
        MB.reset()
        wsT = MB.alloc([2, 8, 128], BF16)
        lng = MB.alloc([1024], F32)
        lnb = MB.alloc([1024], F32)
        bst2 = MB.alloc([2, 6], F32)
        mv2 = MB.alloc([2], F32)
        rs1 = MB.alloc([1], F32)
        mark = MB.p
        wtmp = MB.alloc([2, 8, 128], F32)
        S.dma("sp", lng, bc(lngb[0:1, :], [128, 1024]), "misc1", writes=["lng"])
        S.dma("sp", lnb, bc(lngb[1:2, :], [128, 1024]), "misc2", writes=["lnb"])
        S.dma("sp", wtmp, wsT_d, "misc0", writes=["wtmp"])
        for ty in range(2):
            A("dve", lambda e, ty=ty: e.tensor_tensor(out=wsT[:, ty, :, :], in0=wtmp[:, ty, :, :],
                                                      in1=bc(caus[:, ty:ty + 1, :], [128, 8, 128]), op=ALU.mult),
              reads=["wtmp", "caus"], writes=["wsT"])
        S.barrier()
        MB.p = mark
        gv = MB.alloc([1024], F32)
        vnb = MB.alloc([1024], BF16)
        gu = MB.alloc([1024], F32)
        so = MB.alloc([1024], BF16)
        hv0, hv1, hu0, hu1 = acquire(), acquire(), acquire(), acquire()

        def Pvs(c):
            proj(0, lhs_act(c), hv0, [("actT", c)])
            proj(1, lhs_act(c), hv1, [("actT", c)])

        def Pu(c):
            proj(4, lhs_act(c), hu0, [("actT", c)])
            proj(5, lhs_act(c), hu1, [("actT", c)])

        def LNc(c):
            for i in range(2):
                A("act", lambda e, i=i: e.activation(out=gv[:, i * 512:(i + 1) * 512], in_=pb[i][:, :], func=AF.Gelu),
                  reads=[("pb", i)], writes=["gv"])
            for i in range(2):
                A("dve", lambda e, i=i: e.bn_stats(out=bst2[:, i, :], in_=gv[:, i * 512:(i + 1) * 512]),
                  reads=["gv"], writes=["bst2"])
            A("dve", lambda e: e.bn_aggr(out=mv2, in_=bst2[:, :, :].rearrange("p a b -> p (a b)")),
              reads=["bst2"], writes=["mv2"])
            rstd_from_ss(mv2[:, 1:2], rs1, 1.0, ["mv2"], "rs1")
            A("dve", lambda e: e.tensor_scalar(out=gv, in0=gv, scalar1=mv2[:, 0:1], scalar2=rs1[:, 0:1],
                                               op0=ALU.subtract, op1=ALU.mult), reads=["gv", "mv2", "rs1"], writes=["gv"])
            A("dve", lambda e: e.tensor_tensor(out=gv, in0=gv, in1=lng, op=ALU.mult), reads=["gv", "lng"], writes=["gv"])
            A("dve", lambda e: e.tensor_tensor(out=gv, in0=gv, in1=lnb, op=ALU.add), reads=["gv", "lnb"], writes=["gv"])
            A("act", lambda e: e.activation(out=vnb, in_=gv, func=AF.Copy), reads=["gv"], writes=["vnb"])
            if c == 8:
                S.dma("sp", vn_o, gv, "vno", reads=["gv"], writes=["vno"], is_out=True)

        def SPc(c):
            ty = 0 if c < 8 else 1
            for g in range(8):
                bk = 2 + g // 4
                A("pe", lambda e, g=g, bk=bk, ty=ty: e.matmul(pb[bk][:, (g % 4) * 128:(g % 4 + 1) * 128],
                                                              lhsT=wsT[:, ty, g, :], rhs=vnb[:, g * 128:(g + 1) * 128],
                                                              start=True, stop=True),
                  reads=["wsT", "vnb"], writes=[("pb", bk)])

        def GUc(c):
            for i in range(2):
                A("act", lambda e, i=i: e.activation(out=gu[:, i * 512:(i + 1) * 512], in_=pb[4 + i][:, :], func=AF.Gelu),
                  reads=[("pb", 4 + i)], writes=["gu"])

        def STc(c):
            ty = 0 if c < 8 else 1
            for g in range(8):
                bk = 2 + g // 4
                A("dve", lambda e, g=g, bk=bk, ty=ty: e.scalar_tensor_tensor(
                    out=so[:, g * 128:(g + 1) * 128], in0=pb[bk][:, (g % 4) * 128:(g % 4 + 1) * 128],
                    scalar=bcol[:, ty, g:g + 1], in1=gu[:, g * 128:(g + 1) * 128], op0=ALU.add, op1=ALU.mult),
                  reads=[("pb", bk), "bcol", "gu"], writes=["so"])

        def Tc(c):
            for g in range(8):
                A("pe", lambda e, g=g: e.transpose(pb7[:, g * 128:(g + 1) * 128], so[:, g * 128:(g + 1) * 128], ident_b),
                  reads=["so", "ident_b"], writes=["pb7"])
            A("act", lambda e, c=c: e.activation(out=mixT[:, 8:16, c * 128:(c + 1) * 128],
                                                 in_=pb7[:, :].rearrange("p (h n) -> p h n", h=8), func=AF.Copy),
              reads=["pb7"], writes=[("mixT", c), "mixS_all"])

        Pvs(0); LNc(0); Pu(0); GUc(0)
        for c in range(NCH):
            n = c + 1
            if n < NCH:
                Pvs(n)
                if n == NCH - 1:
                    release(hv0)
                    release(hv1)
            SPc(c)
            STc(c)
            if n < NCH:
                LNc(n)
                Pu(n)
                if n == NCH - 1:
                    release(hu0)
                    release(hu1)
            if n < NCH:
                GUc(n)
            Tc(c)
        S.barrier()

        PB3 = Bump(arena, OFF_P3TMP, ARENA_B)
        xin = PB3.alloc([D], F32)
        xn = PB3.alloc([D], F32)
        gtgP = PB3.alloc([D], F32)
        gtgS = PB3.alloc([D], F32)
        junk = PB3.alloc([512], BF16)
        xn2 = PB3.alloc([D], F32)
        bst3 = PB3.alloc([4, 6], F32)
        mv3 = PB3.alloc([2], F32)
        ss4 = stat[:, 8:12]

        def load_gtg(which, ty, tile, key):
            if ty == 0:
                S.dma("sp", tile, bc(ggd[which, 0:1, :], [128, D]), "g_" + key, reads=[("ggd", 2 + 3 * which)], writes=[key])
            else:
                for s_ in range(16):
                    S.dma("sp", tile[8 * s_:8 * s_ + 8, :], bc(ggd[which, 1 + s_:2 + s_, :], [8, D]), "g_" + key,
                          reads=[("ggd", 2 + 3 * which)], writes=[key if s_ == 0 else (key, s_)])
            S.retoken([key], "g_" + key)

        def post_norm_residual(src_fn, src_keys, xres, xres_key, gt, gt_key, tmp_fn, jk):
            for nb in range(4):
                A("act", lambda e, nb=nb: e.activation(out=jk, in_=src_fn(nb), func=AF.Square,
                                                       accum_out=ss4[:, nb:nb + 1]),
                  reads=[src_keys[nb], "ss4"], writes=["junk", "ss4"])
            A("dve", lambda e: e.tensor_reduce(out=ss, in_=ss4, axis=AX.X, op=ALU.add), reads=["ss4"], writes=["ss"])
            rstd_from_ss(ss, rs, float(D), ["ss"], "rs")
            for nb in range(4):
                sl = slice(nb * 512, (nb + 1) * 512)
                t_ap, t_key = tmp_fn(nb)
                A("dve", lambda e, nb=nb, sl=sl, t_ap=t_ap: e.scalar_tensor_tensor(
                    out=t_ap, in0=src_fn(nb), scalar=rs, in1=gt[:, sl], op0=ALU.mult, op1=ALU.mult),
                  reads=[src_keys[nb], "rs", gt_key], writes=[t_key])
                A("dve", lambda e, sl=sl, t_ap=t_ap: e.tensor_tensor(out=xres[:, sl], in0=xres[:, sl], in1=t_ap, op=ALU.add),
                  reads=[t_key, xres_key], writes=[xres_key])

        how = [acquire() for _ in range(4)]
        load_gtg(0, 0, gtgP, "gtgP")
        load_gtg(0, 1, gtgS, "gtgS")
        xnb = junk_region = None
        xnb = view(arena, OFF_P3TMP + 8192, [D], BF16)
        pb6b = pb[6][:, :].bitcast(BF16)
        mkeys = [("pb", 0), ("pb", 1), ("pb", 2), ("pb", 3)]

        def wo_proj(c):
            for nb in range(4):
                proj(nb, (lambda c: (lambda k: mixT[:, k, c * 128:(c + 1) * 128]))(c), how[nb], [("mixT", c)])

        def wo_part1(c):
            for nb in range(4):
                A("act", lambda e, nb=nb: e.activation(out=junk, in_=pb[nb][:, :], func=AF.Square,
                                                       accum_out=ss4[:, nb:nb + 1]),
                  reads=[mkeys[nb], "ss4"], writes=["junk", "ss4"])

        def wo_part2a(c):
            gt, gt_key = (gtgP, "gtgP") if c < 8 else (gtgS, "gtgS")
            S.dma("sp", xin, xm[c * 128:(c + 1) * 128, :], "xin", writes=["xin"])
            A("dve", lambda e: e.tensor_reduce(out=ss, in_=ss4, axis=AX.X, op=ALU.add), reads=["ss4"], writes=["ss"])
            rstd_from_ss(ss, rs, float(D), ["ss"], "rs")
            for nb in range(4):
                sl = slice(nb * 512, (nb + 1) * 512)
                A("dve", lambda e, nb=nb, sl=sl: e.scalar_tensor_tensor(
                    out=xn2[:, sl], in0=pb[nb][:, :], scalar=rs, in1=gt[:, sl], op0=ALU.mult, op1=ALU.mult),
                  reads=[mkeys[nb], "rs", gt_key], writes=[("xn2", nb)])

        def wo_part2b(c):
            for nb in range(4):
                sl = slice(nb * 512, (nb + 1) * 512)
                A("dve", lambda e, sl=sl: e.tensor_tensor(out=xin[:, sl], in0=xin[:, sl], in1=xn2[:, sl], op=ALU.add),
                  reads=[("xn2", nb), "xin"], writes=["xin"])
            S.dma("sp", x1s[c * 128:(c + 1) * 128, :], xin, "x1st", reads=["xin"], writes=[("x1s", c)])
            A("act", lambda e: e.activation(out=xn2, in_=xin, func=AF.Square, accum_out=ss2),
              reads=["xin"] + [("xn2", nb_) for nb_ in range(4)], writes=[("xn2", nb_) for nb_ in range(4)] + ["ss2"])
            rstd_from_ss(ss2, rs2, float(D), ["ss2"], "rs2")
            A("act", lambda e: e.activation(out=xnb[:, 0:1024], in_=xin[:, 0:1024], func=AF.Copy, scale=rs2),
              reads=["xin", "rs2"], writes=["xnb"])
            A("dve", lambda e: e.tensor_scalar(out=xnb[:, 1024:2048], in0=xin[:, 1024:2048], scalar1=rs2, scalar2=None,
                                               op0=ALU.mult),
              reads=["xin", "rs2"], writes=[("xnb", 1)])

        def wo_T(c):
            for k in range(16):
                dst_ps = pb7 if k < 8 else pb6b
                A("pe", lambda e, k=k, dst_ps=dst_ps: e.transpose(dst_ps[:, (k % 8) * 128:(k % 8 + 1) * 128],
                                                                  xnb[:, k * 128:(k + 1) * 128], ident_b),
                  reads=["xnb", ("xnb", 1), "ident_b"], writes=["pb7" if k < 8 else ("pb", 6)])

        def wo_evac(c):
            ty = 0 if c < 8 else 1
            aT2, bT2 = abT[3], abT[2]
            for k in range(16):
                dkeys = [("actT", c)] if k < 8 else [("actTb", c)]
                src_ps, skey = (pb7, "pb7") if k < 8 else (pb6b, ("pb", 6))
                pv1 = src_ps[:, (k % 8) * 128:(k % 8 + 1) * 128]
                dv1 = actT[:, k, c * 128:(c + 1) * 128]
                if ty == 0 and k < 8:
                    A("act", lambda e, k=k, pv1=pv1, dv1=dv1: e.activation(
                        out=dv1, in_=pv1, func=AF.Identity, scale=aT2[:, k, 0:1], bias=bT2[:, k, 0:1]),
                      reads=[skey, "abT"], writes=dkeys)
                elif ty == 0:
                    A("dve", lambda e, k=k, pv1=pv1, dv1=dv1: e.tensor_scalar(
                        out=dv1, in0=pv1, scalar1=aT2[:, k, 0:1], scalar2=bT2[:, k, 0:1], op0=ALU.mult, op1=ALU.add),
                      reads=[skey, "abT"], writes=dkeys)
                else:
                    pv3 = pv1.rearrange("p (s j) -> p s j", s=16)
                    dv3 = dv1.rearrange("p (s j) -> p s j", s=16)
                    A("dve", lambda e, k=k, pv3=pv3, dv3=dv3: e.tensor_tensor(
                        out=dv3, in0=pv3, in1=bc(aT2[:, k, 1:17].unsqueeze(2), [128, 16, 8]), op=ALU.mult),
                      reads=[skey, "abT"], writes=dkeys)
                    A("dve", lambda e, k=k, dv3=dv3: e.tensor_tensor(
                        out=dv3, in0=dv3, in1=bc(bT2[:, k, 1:17].unsqueeze(2), [128, 16, 8]), op=ALU.add),
                      reads=["abT"] + dkeys, writes=dkeys)

        ss2 = stat[:, 12:13]
        rs2 = stat[:, 13:14]
        wo_proj(0)
        for c in range(NCH):
            wo_part1(c)
            wo_part2a(c)
            if c > 0:
                wo_T(c - 1)
            if c + 1 < NCH:
                wo_proj(c + 1)
                if c + 1 == NCH - 1:
                    for h_ in how:
                        release(h_)
            if c > 0:
                wo_evac(c - 1)
            wo_part2b(c)
        wo_T(NCH - 1)
        wo_evac(NCH - 1)
        S.barrier()

        OFF_FIN = OFF_P3TMP + 2 * 8 * 640 * 2 + 2 * 320 * 4
        FB = Bump(arena, OFF_FIN, ARENA_B)
        xinF = [FB.alloc([D], F32) for _ in range(2)]
        gtgF = FB.alloc([D], F32)
        junkF = pb[1][:, 0:512]
        half_chunks = [list(range(0, 5)), list(range(5, 9))]

        def final_load(g):
            S.dma("sp", xinF[g % 2], x1s[g * 128:(g + 1) * 128, :], "xinF%d" % (g % 2),
                  reads=[("x1s", g)], writes=[("xinF", g % 2)])

        def final_chunk(half, cl, facc, gt, gt_key):
            c = half_chunks[half][cl]
            xb, xkey = xinF[c % 2], ("xinF", c % 2)
            fk = ("facc", cl)
            for nb in range(4):
                A("act", lambda e, nb=nb: e.activation(out=junkF, in_=facc[:, cl, nb * 512:(nb + 1) * 512], func=AF.Square,
                                                       accum_out=ss4[:, nb:nb + 1]),
                  reads=[fk, "ss4"], writes=[("pb", 1), "ss4"])
            A("dve", lambda e: e.tensor_reduce(out=ss, in_=ss4, axis=AX.X, op=ALU.add), reads=["ss4"], writes=["ss"])
            rstd_from_ss(ss, rs, float(D), ["ss"], "rs")
            A("dve", lambda e: e.scalar_tensor_tensor(out=facc[:, cl, :], in0=facc[:, cl, :], scalar=rs, in1=gt,
                                                      op0=ALU.mult, op1=ALU.mult),
              reads=[fk, "rs", gt_key], writes=[fk])
            A("dve", lambda e: e.tensor_tensor(out=xb, in0=xb, in1=facc[:, cl, :], op=ALU.add),
              reads=[fk, xkey], writes=[xkey])
            S.dma("sp", y_o[c * 128:(c + 1) * 128, :], xb, "yst%d" % (c % 2), reads=[xkey], writes=[("y", c)],
                  is_out=True)
            if c + 2 < NCH:
                final_load(c + 2)

        load_gtg(1, 0, gtgF, "gtgF")
        final_load(0)
        final_load(1)
        facc_prev = None
        for half in range(2):
            PB3.reset()
            chunks = half_chunks[half]
            nck = len(chunks)
            tok0 = chunks[0] * 128
            Th = nck * 128
            ntb = 1 if Th <= 512 else 2
            Tb = Th // ntb
            facc = view(arena, OFF_BIG, [nck, D], F32)
            aT = [PB3.alloc([8, Th], BF16) for _ in range(2)]
            rtmp = [PB3.alloc([Tb], F32) for _ in range(2)]
            assert PB3.p <= OFF_FIN
            rot = 0
            fin_after = {1: 0, 3: 1, 5: 2, 6: 3, 7: 4}
            for gk in range(8):
                aTg = aT[gk % 2]
                akey = ("aT", gk % 2)
                for pj in range(2):
                    h = acquire()
                    for jj in range(4):
                        j = pj * 4 + jj
                        b0 = (j % 2) * 2
                        for k in range(16):
                            for tb in range(ntb):
                                A("pe", lambda e, k=k, tb=tb, b0=b0, jj=jj, h=h, Tb=Tb, tok0=tok0: e.matmul(
                                    pb[b0 + tb][:, 0:Tb], lhsT=h[1][:, k, jj * 128:(jj + 1) * 128],
                                    rhs=actT[:, k, tok0 + tb * Tb: tok0 + (tb + 1) * Tb],
                                    start=(k == 0), stop=(k == 15)),
                                  reads=[h[2]] + [("actT", cc_) for cc_ in chunks] + [("actTb", cc_) for cc_ in chunks],
                                  writes=[("pb", b0 + tb)],
                                  sig=(k == 15))
                        for tb in range(ntb):
                            rt = rtmp[(j * ntb + tb) % 2]
                            rkey = ("rtmp", (j * ntb + tb) % 2)
                            A("act", lambda e, tb=tb, b0=b0, Tb=Tb, rt=rt: e.activation(out=rt, in_=pb[b0 + tb][:, 0:Tb], func=AF.Relu),
                              reads=[("pb", b0 + tb)], writes=[rkey])
                            A("dve", lambda e, tb=tb, j=j, aTg=aTg, Tb=Tb, rt=rt: e.tensor_tensor(
                                out=aTg[:, j, tb * Tb:(tb + 1) * Tb], in0=rt, in1=rt, op=ALU.mult),
                              reads=[rkey], writes=[akey])
                        if half == 1 and gk == 0 and j in fin_after:
                            final_chunk(0, fin_after[j], facc_prev, gtgF, "gtgF")
                    release(h)
                for nb in range(4):
                    h = acquire()
                    for cl in range(nck):
                        bk = 4 + rot % 3
                        rot += 1
                        for kk in range(8):
                            A("pe", lambda e, kk=kk, cl=cl, bk=bk, h=h, aTg=aTg: e.matmul(
                                pb[bk][:, :], lhsT=aTg[:, kk, cl * 128:(cl + 1) * 128], rhs=h[1][:, kk, :],
                                start=(kk == 0), stop=(kk == 7)),
                              reads=[h[2], akey], writes=[("pb", bk)], sig=(kk == 7))
                        fv = facc[:, cl, nb * 512:(nb + 1) * 512]
                        if gk == 0:
                            A("act", lambda e, fv=fv, bk=bk: e.activation(out=fv, in_=pb[bk][:, :], func=AF.Copy),
                              reads=[("pb", bk)], writes=[("facc", cl)])
                        else:
                            A("dve", lambda e, fv=fv, bk=bk: e.tensor_tensor(out=fv, in0=fv, in1=pb[bk][:, :], op=ALU.add),
                              reads=[("pb", bk), ("facc", cl)], writes=[("facc", cl)])
                    release(h)
            facc_prev = facc
        S.barrier()
        PB3.reset()
        gtgS2 = PB3.alloc([D], F32)
        load_gtg(1, 1, gtgS2, "gtgS2")
        for cl, c in enumerate(half_chunks[1]):
            final_chunk(1, cl, facc_prev, gtgS2 if c == 8 else gtgF, "gtgS2" if c == 8 else "gtgF")
        S.barrier()

        S.finish()
        with nc.Block() as block:
            S.emit(block)
    return nc


def _consts(core):
    hf = core % 2
    inv = (1.0 / (10000.0 ** (np.arange(0, 128, 2, dtype=np.float32) / np.float32(128)))).astype(np.float32)
    gam = np.array(GAM, dtype=np.float64)
    p = np.arange(128)

    def tab(pos):
        ang = pos.astype(np.float32)[:, :, None] * inv[None, None, :]
        return np.stack([np.cos(ang), np.sin(ang)], axis=2).astype(np.float32)

    posm = np.zeros((128, NCH), dtype=np.int64)
    for c in range(8):
        posm[:, c] = hf * 1024 + c * 128 + p
    posm[:, 8] = 16384 + (p % 8)
    posp = np.zeros((128, 8), dtype=np.int64)
    for c in range(8):
        posp[:, c] = c * 128 + p
    dk = 128.0 ** -0.5
    tsc = np.zeros((128, 6, 8), dtype=np.float64)
    n = p[:, None].astype(np.float64)
    tsc[:, 0] = gam[None] ** n
    tsc[:, 1] = gam[None] ** (-n) * dk
    tsc[:, 2] = gam[None] ** (127.0 - n) * dk
    n8 = (p % 8)[:, None].astype(np.float64)
    tsc[:, 3] = gam[None] ** n8
    tsc[:, 4] = gam[None] ** (-n8) * dk
    tsc[:, 5] = gam[None] ** (7.0 - n8) * dk
    pdec = np.zeros((128, 8, 8), dtype=np.float64)
    if hf == 1:
        for c in range(8):
            pdec[:, c, :] = gam[None] ** (1023.0 - (c * 128 + n)) * dk
    caus = np.zeros((128, 2, 128), dtype=np.float32)
    m, nn = p[:, None], p[None, :]
    caus[:, 0] = (m <= nn)
    caus[:, 1] = (m // 8 == nn // 8) & (m % 8 <= nn % 8)
    bmask = np.zeros((128, 16, 128), dtype=np.float32)
    for s in range(16):
        bmask[:, s, 8 * s:8 * s + 8] = 1.0
    tmask = (p[:, None] // 8 == np.arange(16)[None, :]).astype(np.float32)
    return dict(tabm=tab(posm), tabp=tab(posp), tsc=tsc.astype(np.float32), pdec=pdec.astype(np.float32),
                caus=caus, bmask=bmask, tmask=tmask, ident=np.eye(128, dtype=np.float32))


_NC_CACHE = {}


def kernel(x_prompt, x_sample, state_ret, c_prompt, c_sample, w_ada, b_ada, g_pre_mix, g_post_mix,
           g_pre_ffn, g_post_ffn, w_in, w_s, b_s, ln_g, ln_b, w_o, w_ff1, w_ff2, _cores=None):
    f = lambda a: np.ascontiguousarray(np.asarray(a, dtype=np.float32))
    x_prompt, x_sample, state_ret = f(x_prompt), f(x_sample), f(state_ret)
    c_prompt, c_sample = f(c_prompt), f(c_sample)
    shared = dict(
        w_ada=f(w_ada)[0], b_ada=f(b_ada), w_in=f(w_in)[0], w_o=f(w_o)[0], w_ff1=f(w_ff1)[0], w_ff2=f(w_ff2)[0],
        gvec=np.ascontiguousarray(np.concatenate([f(g_pre_mix), f(g_post_mix), f(g_pre_ffn), f(g_post_ffn)], axis=0)),
        lngb=np.ascontiguousarray(np.concatenate([f(ln_g), f(ln_b)], axis=0)),
    )
    ws = f(w_s)[0]
    bs = f(b_s)[0]
    wsT = np.zeros((128, 2, 8, 128), dtype=np.float32)
    wsT[:, 0] = ws.transpose(2, 0, 1)
    blk = ws[:, :8, :8].transpose(2, 0, 1)
    for a in range(16):
        wsT[8 * a:8 * a + 8, 1, :, 8 * a:8 * a + 8] = blk
    bcol = np.zeros((128, 2, 8), dtype=np.float32)
    bcol[:, 0] = bs.T
    bcol[:, 1] = np.tile(bs[:, :8].T, (16, 1))
    shared["wsT"] = wsT
    shared["bcol"] = bcol

    cores = list(range(8)) if _cores is None else _cores
    in_maps = []
    for core in cores:
        b, hf = core // 2, core % 2
        xs = x_sample[16 * core:16 * core + 16].reshape(128, D)
        m = dict(shared)
        m["xm"] = np.ascontiguousarray(np.concatenate([x_prompt[b, hf * 1024:(hf + 1) * 1024], xs], axis=0))
        m["xp"] = np.ascontiguousarray(x_prompt[b, 0:1024])
        m["cc"] = np.ascontiguousarray(np.concatenate([c_prompt[b:b + 1], c_sample[16 * core:16 * core + 16]], axis=0))
        m["st"] = np.ascontiguousarray(state_ret[0, 16 * core:16 * core + 16])
        m.update(_consts(core))
        in_maps.append(m)

    if "nc" not in _NC_CACHE:
        _NC_CACHE["nc"] = build_program()
    nc = _NC_CACHE["nc"]
    res = run_bass_kernel_spmd(nc, in_maps, core_ids=list(range(len(cores))))
    outs = res.results
    if _cores is not None:
        return outs
    y_prompt = np.zeros((4, 2048, D), dtype=np.float32)
    y_sample = np.zeros((128, 8, D), dtype=np.float32)
    sp = np.zeros((1, 4, 8, 128, 128), dtype=np.float32)
    ssn = np.zeros((1, 128, 8, 128, 128), dtype=np.float32)
    vn = np.zeros((1, 128, 8, 1024), dtype=np.float32)
    for core in range(8):
        b, hf = core // 2, core % 2
        r = outs[core]
        y_prompt[b, hf * 1024:(hf + 1) * 1024] = r["y"][0:1024]
        y_sample[16 * core:16 * core + 16] = r["y"][1024:1152].reshape(16, 8, D)
        if hf == 1:
            sp[0, b] = r["sp_out"]
        ssn[0, 16 * core:16 * core + 16] = r["ss_out"]
        vn[0, 16 * core:16 * core + 16] = r["vn_out"].reshape(16, 8, 1024)
    return (y_prompt, y_sample, sp, ssn, vn)
```

```python
import contextlib
import numpy as np
import concourse.bass as bass
import concourse.mybir as mybir
from concourse.bass_utils import run_bass_kernel_spmd

F32 = mybir.dt.float32
BF16 = mybir.dt.bfloat16
AF = mybir.ActivationFunctionType
ALU = mybir.AluOpType
AX = mybir.AxisListType

D = 2048
NCH = 9
T = NCH * 128
EPS = 1e-6
NSLOT = 5
SLOT_B = 16384
ARENA_B = 212800
OFF_RING = 0
OFF_ACT = NSLOT * SLOT_B
OFF_BIG = OFF_ACT + 36864
BIG_B = 40960
OFF_CONST = OFF_BIG + BIG_B
GAM = [1.0 - 2.0 ** (-5.0 - h) for h in range(8)]


class Sched:
    def __init__(self, nc, stack):
        self.nc = nc
        self.stack = stack
        self.engs = ["pe", "act", "dve", "pool", "sp"]
        self.stream = {e: [] for e in self.engs}
        self.cnt = {e: 0 for e in self.engs}
        self.pend = {e: False for e in self.engs}
        self.lastw = {}
        self.readers = {}
        self.waited = {e: {} for e in self.engs}
        self.sems = {}
        self.dcnt = {}
        self.bar = {}
        self.prefetch_sems = set()
        self.out_tokens = {}
        for e in ["pe", "act", "dve"]:
            self._sem(e)

    def _sem(self, name):
        if name not in self.sems:
            self.sems[name] = self.stack.enter_context(self.nc.semaphore("s_" + name))
        return self.sems[name]

    def _deps(self, eng, reads, writes, nobar):
        deps = {}

        def need(tok):
            if tok is None:
                return
            s, v = tok
            if deps.get(s, 0) < v:
                deps[s] = v

        for k in reads:
            need(self.lastw.get(k))
        for k in writes:
            need(self.lastw.get(k))
            for t in self.readers.get(k, ()):
                need(t)
        if not nobar:
            for s, v in self.bar.items():
                need((s, v))
        waits = []
        for s, v in deps.items():
            if s == eng and eng == "pe":
                continue
            if self.waited[eng].get(s, 0) < v:
                self.waited[eng][s] = v
                waits.append((s, v))
        return waits

    def _commit(self, tok, reads, writes):
        for k in reads:
            self.readers.setdefault(k, []).append(tok)
        for k in writes:
            self.lastw[k] = tok
            self.readers[k] = []

    def add(self, eng, fn, reads=(), writes=(), sig=True, nobar=False):
        waits = self._deps(eng, reads, writes, nobar)
        if sig:
            self.cnt[eng] += 1
            tok = (eng, self.cnt[eng])
            self.pend[eng] = False
        else:
            tok = (eng, self.cnt[eng] + 1)
            self.pend[eng] = True
        self.stream[eng].append((waits, fn, eng if sig else None, 1))
        self._commit(tok, reads, writes)
        return tok

    def dma(self, q, out, in_, sem, reads=(), writes=(), nobar=False, is_out=False):
        waits = self._deps(q, reads, writes, nobar)
        self._sem(sem)
        self.dcnt[sem] = self.dcnt.get(sem, 0) + 1
        tok = (sem, 16 * self.dcnt[sem])
        self.stream[q].append((waits, lambda e: e.dma_start(out=out, in_=in_), sem, 16))
        self._commit(tok, reads, writes)
        if is_out:
            self.out_tokens[sem] = tok[1]
        return tok

    def retoken(self, keys, sem):
        tok = (sem, 16 * self.dcnt[sem])
        for k in keys:
            self.lastw[k] = tok

    def barrier(self):
        for e in ["pe", "act", "dve"]:
            assert not self.pend[e], e
            if self.cnt[e] > 0:
                self.bar[e] = self.cnt[e]
        for s, n in self.dcnt.items():
            if s not in self.prefetch_sems:
                self.bar[s] = 16 * n

    def finish(self):
        waits = []
        for s, v in self.out_tokens.items():
            waits.append((s, v))
        for e in ["pe", "act", "dve"]:
            waits.append((e, self.cnt[e]))
        self.stream["sp"].append((waits, None, None, 0))

    def emit(self, block):
        def run(name, eng):
            for waits, fn, sem, inc in self.stream[name]:
                for s, v in waits:
                    eng.wait_ge(self.sems[s], v)
                if fn is not None:
                    ins = fn(eng)
                    if sem is not None:
                        ins.then_inc(self.sems[sem], inc)

        @block.tensor
        def _(e):
            run("pe", e)

        @block.scalar
        def _(e):
            run("act", e)

        @block.vector
        def _(e):
            run("dve", e)

        @block.gpsimd
        def _(e):
            run("pool", e)

        @block.sync
        def _(e):
            run("sp", e)


class Bump:
    def __init__(self, arena, base, limit):
        self.arena, self.base, self.limit, self.p = arena, base, limit, base

    def reset(self, base=None, limit=None):
        if base is not None:
            self.base = base
        if limit is not None:
            self.limit = limit
        self.p = self.base

    def alloc(self, shape, dt, parts=128):
        n = int(np.prod(shape))
        nb = n * (4 if dt == F32 else 2)
        nb_al = (nb + 63) // 64 * 64
        off = self.p
        assert off + nb_al <= self.limit, (off, nb_al, self.limit)
        self.p += nb_al
        self.last = off
        return view(self.arena, off, shape, dt, parts)


def view(arena, off, shape, dt, parts=128):
    n = int(np.prod(shape))
    assert off % 4 == 0
    if dt == F32:
        v = arena[:, off // 2: off // 2 + 2 * n].bitcast(F32)
    else:
        v = arena[:, off // 2: off // 2 + n]
    if parts != 128:
        v = v[0:parts]
    if len(shape) == 2:
        v = v.rearrange("p (a b) -> p a b", a=shape[0])
    elif len(shape) == 3:
        v = v.rearrange("p (a b c) -> p a b c", a=shape[0], b=shape[1])
    elif len(shape) == 4:
        v = v.rearrange("p (a b c d) -> p a b c d", a=shape[0], b=shape[1], c=shape[2])
    return v


def bc(ap, shape):
    return ap.broadcast_to(list(shape))


def build_program():
    nc = bass.Bass("TRN2", target_bir_lowering=False)

    def din(name, shape):
        return nc.dram_tensor(name, list(shape), F32, kind="ExternalInput").ap()

    def dout(name, shape):
        return nc.dram_tensor(name, list(shape), F32, kind="ExternalOutput").ap()

    xm = din("xm", [T, D])
    xp = din("xp", [1024, D])
    cc = din("cc", [17, D])
    st = din("st", [16, 8, 128, 128])
    w_ada = din("w_ada", [D, 6 * D])
    b_ada = din("b_ada", [1, 6 * D])
    gvec = din("gvec", [4, D])
    w_in = din("w_in", [D, 6144])
    w_o = din("w_o", [D, D])
    w_ff1 = din("w_ff1", [D, 4 * D])
    w_ff2 = din("w_ff2", [4 * D, D])
    lngb = din("lngb", [2, 1024])
    wsT_d = din("wsT", [128, 2, 8, 128])
    bcol_d = din("bcol", [128, 2, 8])
    tabm_d = din("tabm", [128, NCH, 2, 64])
    tabp_d = din("tabp", [128, 8, 2, 64])
    tsc_d = din("tsc", [128, 6, 8])
    pdec_d = din("pdec", [128, 8, 8])
    caus_d = din("caus", [128, 2, 128])
    bmask_d = din("bmask", [128, 16, 128])
    tmask_d = din("tmask", [128, 16])
    ident_d = din("ident", [128, 128])

    y_o = dout("y", [T, D])
    sp_o = dout("sp_out", [8, 128, 128])
    ss_o = dout("ss_out", [16, 8, 128, 128])
    vn_o = dout("vn_out", [128, 1024])
    x1s = nc.dram_tensor("x1s", [T, D], F32, kind="Internal").ap()
    ggd = nc.dram_tensor("ggd", [2, 17, D], F32, kind="Internal").ap()

    stack = contextlib.ExitStack()
    with stack:
        arena = stack.enter_context(nc.sbuf_tensor("arena", [128, ARENA_B // 2], BF16))
        pb = [stack.enter_context(nc.psum_tensor("pb%d" % i, [128, 512], F32)) for i in range(7)]
        pb7 = stack.enter_context(nc.psum_tensor("pb7", [128, 1024], BF16))
        S = Sched(nc, stack)
        A = S.add

        ringv = [view(arena, OFF_RING + i * SLOT_B, [16, 512], BF16) for i in range(NSLOT)]
        ringv8 = [view(arena, OFF_RING + i * SLOT_B, [8, 512], BF16) for i in range(NSLOT)]
        actT = view(arena, OFF_ACT, [16, T], BF16)
        mixT = view(arena, OFF_BIG, [16, T], BF16)
        CB = Bump(arena, OFF_CONST, ARENA_B)
        ident_f = CB.alloc([128], F32)
        ident_b = CB.alloc([128], BF16)
        abT = [CB.alloc([16, 17], F32) for _ in range(4)]
        stat = CB.alloc([64], F32)
        epsc = stat[:, 32:33]
        OFF_P3TMP = CB.p
        scT = CB.alloc([16, 17], BF16)
        badap = CB.alloc([512], F32, parts=17)
        gbp = CB.alloc([512], F32, parts=17)
        secp = CB.alloc([512], F32, parts=17)
        tabm = CB.alloc([NCH, 2, 64], F32)
        tsc = CB.alloc([6, 8], F32)
        bcol = CB.alloc([2, 8], F32)
        tmask = CB.alloc([16], F32)
        caus = CB.alloc([2, 128], BF16)
        S_f = CB.alloc([8, 128], F32)
        Sg_b = CB.alloc([8, 128], BF16)
        OFF_MIXTMP = CB.p
        bmask = view(arena, OFF_BIG + 36864, [16, 128], BF16)

        panels = []
        wav = w_ada.rearrange("(k p) n -> p k n", p=128)
        wiv = w_in.rearrange("(k p) n -> p k n", p=128)
        wov = w_o.rearrange("(k p) n -> p k n", p=128)
        w1v = w_ff1.rearrange("(k p) n -> p k n", p=128)
        w2v = w_ff2.rearrange("(k p) n -> p k n", p=128)
        for i in range(8):
            panels.append((wav[:, :, i * 512:(i + 1) * 512], 16))
        for c0 in (1024, 1536, 2048, 2560):
            panels.append((wiv[:, :, c0:c0 + 512], 16))
        for gi_, grp in enumerate(((1024, 2048, 0, 3072), (1536, 2560, 512, 3584), (5120, 5632, 4096, 4608))):
            for c0 in grp:
                panels.append((wiv[:, :, c0:c0 + 512], 16))
            if gi_ < 2:
                for i in range(8 + gi_ * 8, 16 + gi_ * 8):
                    panels.append((wav[:, :, i * 512:(i + 1) * 512], 16))
        for nb in range(4):
            panels.append((wov[:, :, nb * 512:(nb + 1) * 512], 16))
        for half in range(2):
            for gk in range(8):
                for pj in range(2):
                    c0 = gk * 1024 + pj * 512
                    panels.append((w1v[:, :, c0:c0 + 512], 16))
                for nb in range(4):
                    panels.append((w2v[:, gk * 8:(gk + 1) * 8, nb * 512:(nb + 1) * 512], 8))

        ring = {"next_dma": 0, "next_acq": 0, "free": list(range(NSLOT)), "slot_of": {}}
        for i in range(NSLOT):
            S.prefetch_sems.add("ring%d" % i)

        def ring_pump():
            while ring["next_dma"] < len(panels) and ring["free"]:
                i = ring["next_dma"]
                slot = ring["free"].pop(0)
                ap, nk = panels[i]
                dst = ringv[slot] if nk == 16 else ringv8[slot]
                S.dma("pool", dst, ap, "ring%d" % slot, writes=[("ring", slot)], nobar=True)
                ring["slot_of"][i] = slot
                ring["next_dma"] += 1

        def acquire():
            i = ring["next_acq"]
            ring["next_acq"] += 1
            if i not in ring["slot_of"]:
                ring_pump()
            assert i in ring["slot_of"], "ring overflow"
            slot = ring["slot_of"][i]
            nk = panels[i][1]
            return (slot, ringv[slot] if nk == 16 else ringv8[slot], ("ring", slot))

        def release(h):
            ring["free"].append(h[0])
            ring_pump()

        ckeys = []

        def cload(dst, src, key):
            S.dma("sp", dst, src, "const", writes=[key])
            ckeys.append(key)

        cload(ident_f, ident_d, "ident_f")
        cload(tabm, tabm_d, "tabm")
        cload(tsc, tsc_d, "tsc")
        cload(bcol, bcol_d, "bcol")
        cload(tmask, tmask_d, "tmask")
        S.retoken(ckeys, "const")
        A("dve", lambda e: e.memset(epsc, EPS), writes=["epsc"])
        pkeys = []
        for dst, src, key in ((ident_b, ident_d, "ident_b"), (caus, caus_d, "caus")):
            S.dma("pool", dst, src, "constb", writes=[key])
            pkeys.append(key)
        S.retoken(pkeys, "constb")
        ring_pump()

        def rstd_from_ss(ss_ap, out_ap, n, rk, wk):
            A("act", lambda e: e.activation(out=out_ap, in_=ss_ap, func=AF.Sqrt, scale=1.0 / n, bias=epsc),
              reads=rk + ["epsc"], writes=[wk])
            A("dve", lambda e: e.reciprocal(out=out_ap, in_=out_ap), reads=[wk], writes=[wk])

        def norm_rows(xin, xin_key, xn, xn_key, ss, ss_key, rs, rs_key):
            A("act", lambda e: e.activation(out=xn, in_=xin, func=AF.Square, accum_out=ss),
              reads=[xin_key, ss_key], writes=[xn_key, ss_key])
            rstd_from_ss(ss, rs, float(D), [ss_key], rs_key)
            A("act", lambda e: e.activation(out=xn, in_=xin, func=AF.Copy, scale=rs),
              reads=[xin_key, rs_key], writes=[xn_key])

        def transpose_mod(xn, xn_key, dst, dst_keys, aT, bT, ty, bank0=5):
            for q4 in range(4):
                bk = bank0 + (q4 % 2)
                for kk in range(4):
                    k = q4 * 4 + kk
                    A("pe", lambda e, k=k, kk=kk, bk=bk: e.transpose(
                        pb[bk][:, kk * 128:(kk + 1) * 128], xn[:, k * 128:(k + 1) * 128], ident_f),
                      reads=[xn_key, "ident_f"], writes=[("pb", bk)])
                pv = pb[bk][:, :].rearrange("p (a b) -> p a b", a=4)
                dv = dst[:, q4 * 4:(q4 + 1) * 4, :]
                if ty == 0 and q4 % 2 == 0:
                    for kk in range(4):
                        k = q4 * 4 + kk
                        A("act", lambda e, k=k, kk=kk, bk=bk: e.activation(
                            out=dst[:, k, :], in_=pb[bk][:, kk * 128:(kk + 1) * 128], func=AF.Identity,
                            scale=aT[:, k, 0:1], bias=bT[:, k, 0:1]),
                          reads=[("pb", bk), "abT"], writes=dst_keys)
                elif ty == 0:
                    a_b = bc(aT[:, q4 * 4:(q4 + 1) * 4, 0:1], [128, 4, 128])
                    b_b = bc(bT[:, q4 * 4:(q4 + 1) * 4, 0:1], [128, 4, 128])
                    A("dve", lambda e, pv=pv, a_b=a_b: e.tensor_tensor(out=pv, in0=pv, in1=a_b, op=ALU.mult),
                      reads=[("pb", bk), "abT"], writes=[("pb", bk)])
                    A("dve", lambda e, pv=pv, b_b=b_b, dv=dv: e.tensor_tensor(out=dv, in0=pv, in1=b_b, op=ALU.add),
                      reads=[("pb", bk), "abT"], writes=dst_keys)
                else:
                    for kk in range(4):
                        k = q4 * 4 + kk
                        pv1 = pb[bk][:, kk * 128:(kk + 1) * 128].rearrange("p (s j) -> p s j", s=16)
                        dv1 = dst[:, k, :].rearrange("p (s j) -> p s j", s=16)
                        a_b = bc(aT[:, k, 1:17].unsqueeze(2), [128, 16, 8])
                        b_b = bc(bT[:, k, 1:17].unsqueeze(2), [128, 16, 8])
                        A("dve", lambda e, pv1=pv1, a_b=a_b: e.tensor_tensor(out=pv1, in0=pv1, in1=a_b, op=ALU.mult),
                          reads=[("pb", bk), "abT"], writes=[("pb", bk)])
                        A("dve", lambda e, pv1=pv1, b_b=b_b, dv1=dv1: e.tensor_tensor(out=dv1, in0=pv1, in1=b_b, op=ALU.add),
                          reads=[("pb", bk), "abT"], writes=dst_keys)

        def proj(bank, lhs_fn, h, lhs_keys, n=512):
            for k in range(16):
                A("pe", lambda e, k=k: e.matmul(pb[bank][:, 0:n], lhsT=lhs_fn(k), rhs=h[1][:, k, 0:n],
                                                start=(k == 0), stop=(k == 15)),
                  reads=lhs_keys + [h[2]], writes=[("pb", bank)], sig=(k == 15))

        def rotary(bank, tab, c, out4, out_key, tA, tB):
            pv = pb[bank][:, :].rearrange("p (h t d) -> p h t d", h=4, t=2)
            x1, x2 = pv[:, :, 0, :], pv[:, :, 1, :]
            cos = bc(tab[:, c, 0:1, :], [128, 4, 64])
            sin = bc(tab[:, c, 1:2, :], [128, 4, 64])
            rk = [("pb", bank), "tab"]
            A("dve", lambda e: e.tensor_tensor(out=tA, in0=x1, in1=cos, op=ALU.mult), reads=rk, writes=["rtA"])
            A("dve", lambda e: e.tensor_tensor(out=tB, in0=x2, in1=sin, op=ALU.mult), reads=rk, writes=["rtB"])
            A("dve", lambda e: e.tensor_tensor(out=out4[:, :, 0, :], in0=tA, in1=tB, op=ALU.subtract),
              reads=["rtA", "rtB"], writes=[out_key])
            A("dve", lambda e: e.tensor_tensor(out=tA, in0=x1, in1=sin, op=ALU.mult), reads=rk, writes=["rtA"])
            A("dve", lambda e: e.tensor_tensor(out=tB, in0=x2, in1=cos, op=ALU.mult), reads=rk, writes=["rtB"])
            A("dve", lambda e: e.tensor_tensor(out=out4[:, :, 1, :], in0=tA, in1=tB, op=ALU.add),
              reads=["rtA", "rtB"], writes=[out_key])

        TB = Bump(arena, OFF_BIG, OFF_BIG + BIG_B)
        cc_sb = TB.alloc([D], F32, parts=17)
        hTp_all = TB.alloc([16, 1024], BF16)
        S.dma("sp", cc_sb, cc, "misc0", writes=["cc_sb"])
        A("act", lambda e: e.activation(out=cc_sb, in_=cc_sb, func=AF.Silu), reads=["cc_sb"], writes=["cc_sb"])
        for k in range(16):
            A("pe", lambda e, k=k: e.transpose(pb[5][:, k * 17:(k + 1) * 17], cc_sb[:, k * 128:(k + 1) * 128],
                                               ident_f[0:17, 0:17]),
              reads=["cc_sb", "ident_f"], writes=[("pb", 5)])
        A("act", lambda e: e.activation(out=scT, in_=pb[5][:, 0:272].rearrange("p (a b) -> p a b", a=16), func=AF.Copy),
          reads=[("pb", 5)], writes=["scT"])

        def mod_a(i, mbank=4):
            s_, nbk = i // 4, i % 4
            cs = slice(s_ * D + nbk * 512, s_ * D + (nbk + 1) * 512)
            S.dma("sp", badap, bc(b_ada[0:1, cs], [17, 512]), "misc1", writes=["badap"])
            if s_ in (1, 2, 4, 5):
                gi = {1: 0, 2: 1, 4: 2, 5: 3}[s_]
                S.dma("sp", gbp, bc(gvec[gi:gi + 1, nbk * 512:(nbk + 1) * 512], [17, 512]), "misc2", writes=["gbp"])
            h = acquire()
            for k in range(16):
                A("pe", lambda e, k=k, h=h: e.matmul(pb[mbank][0:17, :], lhsT=scT[:, k, :], rhs=h[1][:, k, :],
                                                     start=(k == 0), stop=(k == 15)),
                  reads=["scT", h[2]], writes=[("pb", mbank)], sig=(k == 15))
            release(h)
            A("dve", lambda e: e.tensor_tensor(out=secp, in0=pb[mbank][0:17, :], in1=badap, op=ALU.add),
              reads=[("pb", mbank), "badap"], writes=["secp"])
            if s_ in (1, 4):
                A("dve", lambda e: e.scalar_tensor_tensor(out=secp, in0=secp, scalar=1.0, in1=gbp,
                                                          op0=ALU.add, op1=ALU.mult),
                  reads=["secp", "gbp"], writes=["secp"])
            if s_ in (2, 5):
                A("dve", lambda e: e.tensor_tensor(out=secp, in0=secp, in1=gbp, op=ALU.mult),
                  reads=["secp", "gbp"], writes=["secp"])

        def mod_b(i, tbank=6):
            s_, nbk = i // 4, i % 4
            if s_ in (2, 5):
                S.dma("sp", ggd[0 if s_ == 2 else 1][:, nbk * 512:(nbk + 1) * 512], secp, "misc3_%d" % s_,
                      reads=["secp"], writes=[("ggd", s_)])
            else:
                dstT = abT[{0: 0, 1: 1, 3: 2, 4: 3}[s_]]
                for kk in range(4):
                    A("pe", lambda e, kk=kk: e.transpose(pb[tbank][:, kk * 17:(kk + 1) * 17],
                                                         secp[:, kk * 128:(kk + 1) * 128], ident_f[0:17, 0:17]),
                      reads=["secp", "ident_f"], writes=[("pb", tbank)])
                A("act", lambda e, dstT=dstT, nbk=nbk: e.activation(
                    out=dstT[:, nbk * 4:(nbk + 1) * 4, :],
                    in_=pb[tbank][:, 0:68].rearrange("p (a b) -> p a b", a=4), func=AF.Copy),
                  reads=[("pb", tbank)], writes=["abT"])

        def mod_panel(i):
            mod_a(i)
            mod_b(i)

        HB = Bump(arena, OFF_MIXTMP, ARENA_B)
        xinH = [HB.alloc([D], F32) for _ in range(2)]
        xnH = [HB.alloc([D], BF16) for _ in range(2)]
        ss = stat[:, 0:1]
        rs = stat[:, 1:2]
        ssH = [stat[:, 2:3], stat[:, 3:4]]
        rsH = [stat[:, 4:5], stat[:, 5:6]]
        ssD = [stat[:, 6:7], stat[:, 7:8]]

        def Ha(i):
            j = i % 2
            xb, xkey = xinH[j], ("xinH", j)
            xn_, nkey = xnH[j], ("xnH", j)
            src = xp[i * 128:(i + 1) * 128, :] if i < 8 else xm[(i - 8) * 128:(i - 7) * 128, :]
            S.dma("sp", xb, src, "xinH%d" % j, writes=[xkey])
            A("act", lambda e: e.activation(out=xn_[:, 0:1024], in_=xb[:, 0:1024], func=AF.Square, accum_out=ssH[j]),
              reads=[xkey], writes=[nkey, ("ssH", j)])
            A("dve", lambda e: e.memset(ssD[j], 0.0), writes=[("ssD", j)])
            A("dve", lambda e: e.scalar_tensor_tensor(out=xn_[:, 1024:2048], in0=xb[:, 1024:2048], scalar=1.0,
                                                      in1=xb[:, 1024:2048], op0=ALU.mult, op1=ALU.mult, accum_out=ssD[j]),
              reads=[xkey, ("ssD", j)], writes=[(nkey, 1), ("ssD", j)])
            A("dve", lambda e: e.tensor_tensor(out=ssH[j], in0=ssH[j], in1=ssD[j], op=ALU.add),
              reads=[("ssH", j), ("ssD", j)], writes=[("ssH", j)])
            rstd_from_ss(ssH[j], rsH[j], float(D), [("ssH", j)], ("rsH", j))
            A("act", lambda e: e.activation(out=xn_[:, 0:1024], in_=xb[:, 0:1024], func=AF.Copy, scale=rsH[j]),
              reads=[xkey, ("rsH", j)], writes=[nkey])
            A("dve", lambda e: e.tensor_scalar(out=xn_[:, 1024:2048], in0=xb[:, 1024:2048], scalar1=rsH[j], scalar2=None,
                                               op0=ALU.mult),
              reads=[xkey, ("rsH", j)], writes=[(nkey, 1)])

        def Hb(i):
            j = i % 2
            xn_, nkey = xnH[j], ("xnH", j)
            for q8 in range(2):
                for kk in range(8):
                    k = q8 * 8 + kk
                    A("pe", lambda e, k=k, kk=kk: e.transpose(
                        pb7[:, kk * 128:(kk + 1) * 128], xn_[:, k * 128:(k + 1) * 128], ident_b),
                      reads=[nkey, (nkey, 1), "ident_b"], writes=["pb7"])
                pv = pb7[:, :].rearrange("p (a b) -> p a b", a=8)
                if i < 8:
                    dv, dkey = hTp_all[:, q8 * 8:(q8 + 1) * 8, i * 128:(i + 1) * 128], "hTp_all"
                else:
                    dv, dkey = actT[:, q8 * 8:(q8 + 1) * 8, (i - 8) * 128:(i - 7) * 128], ("actT", i - 8)
                A("dve", lambda e, pv=pv, dv=dv: e.tensor_copy(out=dv, in_=pv), reads=["pb7"], writes=[dkey])

        mi = 0
        Ha(0)
        for i in range(17):
            if i + 1 < 17:
                Ha(i + 1)
            Hb(i)
            if i % 2 == 1 and mi < 8:
                mod_panel(mi)
                mi += 1
        while mi < 8:
            mod_panel(mi)
            mi += 1
        S.barrier()
        for k in range(16):
            if k % 2 == 0:
                A("act", lambda e, k=k: e.activation(out=hTp_all[:, k, :], in_=hTp_all[:, k, :], func=AF.Identity,
                                                     scale=abT[1][:, k, 0:1], bias=abT[0][:, k, 0:1]),
                  reads=["hTp_all", "abT"], writes=["hTp_all"])
            else:
                A("dve", lambda e, k=k: e.tensor_scalar(out=hTp_all[:, k, :], in0=hTp_all[:, k, :],
                                                        scalar1=abT[1][:, k, 0:1], scalar2=abT[0][:, k, 0:1],
                                                        op0=ALU.mult, op1=ALU.add),
                  reads=["hTp_all", "abT"], writes=["hTp_all"])

        def modulate_main():
            for k in range(16):
                if k % 2 == 1:
                    A("act", lambda e, k=k: e.activation(out=actT[:, k, 0:1024], in_=actT[:, k, 0:1024], func=AF.Identity,
                                                         scale=abT[1][:, k, 0:1], bias=abT[0][:, k, 0:1]),
                      reads=[("actT", c_) for c_ in range(8)] + ["abT"], writes=[("actT", c_) for c_ in range(8)])
                else:
                    A("dve", lambda e, k=k: e.tensor_scalar(out=actT[:, k, 0:1024], in0=actT[:, k, 0:1024],
                                                            scalar1=abT[1][:, k, 0:1], scalar2=abT[0][:, k, 0:1],
                                                            op0=ALU.mult, op1=ALU.add),
                      reads=[("actT", c_) for c_ in range(8)] + ["abT"], writes=[("actT", c_) for c_ in range(8)])
                sv = actT[:, k, 1024:1152].rearrange("p (s j) -> p s j", s=16)
                A("dve", lambda e, k=k, sv=sv: e.tensor_tensor(out=sv, in0=sv, in1=bc(abT[1][:, k, 1:17].unsqueeze(2), [128, 16, 8]),
                                                               op=ALU.mult),
                  reads=[("actT", 8), "abT"], writes=[("actT", 8)])
                A("dve", lambda e, k=k, sv=sv: e.tensor_tensor(out=sv, in0=sv, in1=bc(abT[0][:, k, 1:17].unsqueeze(2), [128, 16, 8]),
                                                               op=ALU.add),
                  reads=[("actT", 8), "abT"], writes=[("actT", 8)])

        HB.reset()
        tabp = HB.alloc([8, 2, 64], F32)
        pdec = HB.alloc([8, 8], F32)
        krot1 = HB.alloc([4, 2, 64], F32)
        rtA1 = HB.alloc([4, 64], F32)
        rtB1 = HB.alloc([4, 64], F32)
        kdp = [[HB.alloc([512], BF16) for _ in range(2)] for _ in range(2)]
        vtp = [[HB.alloc([512], BF16) for _ in range(2)] for _ in range(2)]
        zer = HB.alloc([512], BF16)
        S.dma("sp", tabp, tabp_d, "misc0", writes=["tab"])
        S.dma("sp", pdec, pdec_d, "misc1", writes=["pdec"])
        hp = [acquire() for _ in range(4)]
        A("dve", lambda e: e.memset(zer, 0.0), writes=["zer"])
        for g4 in range(2):
            A("pe", lambda e, g4=g4: e.matmul(pb[4 + g4][:, :], lhsT=zer[:, 0:128], rhs=zer, start=True, stop=False),
              reads=["zer"], writes=[("pb", 4 + g4)])

        def prevP(pc, i):
            proj(i, lambda k: hTp_all[:, k, pc * 128:(pc + 1) * 128], hp[i], ["hTp_all"])
            pb_ = pc % 2
            if i < 2:
                rotary(i, tabp, pc, krot1, "krot", rtA1, rtB1)
                A("dve", lambda e: e.tensor_tensor(
                    out=kdp[pb_][i][:, :].rearrange("p (h d) -> p h d", h=4),
                    in0=krot1[:, :, :, :].rearrange("p h t d -> p h (t d)"),
                    in1=bc(pdec[:, pc, i * 4:(i + 1) * 4].unsqueeze(2), [128, 4, 128]), op=ALU.mult),
                  reads=["krot", "pdec"], writes=[("kdp", pb_, i)])
            else:
                A("act", lambda e: e.activation(out=vtp[pb_][i - 2], in_=pb[i][:, :], func=AF.Copy),
                  reads=[("pb", i)], writes=[("vtp", pb_, i - 2)])

        def prevS(pc):
            pb_ = pc % 2
            for hd in range(8):
                g4, hh = hd // 4, hd % 4
                A("pe", lambda e, g4=g4, hh=hh: e.matmul(
                    pb[4 + g4][:, hh * 128:(hh + 1) * 128], lhsT=kdp[pb_][g4][:, hh * 128:(hh + 1) * 128],
                    rhs=vtp[pb_][g4][:, hh * 128:(hh + 1) * 128], start=False, stop=(pc == 7 and hh == 3)),
                  reads=[("kdp", pb_, g4), ("vtp", pb_, g4)], writes=[("pb", 4 + g4)])

        for i in range(4):
            prevP(0, i)
        modulate_main()
        for pc in range(8):
            if pc + 1 < 8:
                prevP(pc + 1, 0)
            prevS(pc)
            if pc + 1 < 8:
                for i in range(1, 4):
                    prevP(pc + 1, i)
        for i in range(4):
            release(hp[i])
        for g4 in range(2):
            A("dve", lambda e, g4=g4: e.tensor_copy(
                out=S_f[:, g4 * 4:(g4 + 1) * 4, :], in_=pb[4 + g4][:, :].rearrange("p (h d) -> p h d", h=4)),
              reads=[("pb", 4 + g4)], writes=[("S_f", g4 * 4 + i_) for i_ in range(4)])
            for hh in range(4):
                hd = g4 * 4 + hh
                A("act", lambda e, hd=hd: e.activation(out=Sg_b[:, hd, :], in_=S_f[:, hd, :], func=AF.Copy,
                                                       scale=float(GAM[hd])),
                  reads=[("S_f", hd)], writes=[("Sg_b", hd)])
        S.barrier()
        S.dma("pool", bmask, bmask_d, "bmaskl", writes=["bmask"])

        MB = Bump(arena, OFF_MIXTMP, ARENA_B)
        krot = MB.alloc([4, 2, 64], F32)
        off_krot = MB.last
        qrot = MB.alloc([4, 2, 64], F32)
        off_qrot = MB.last
        rtA = MB.alloc([4, 64], F32)
        off_rtA = MB.last
        rtB = MB.alloc([4, 64], F32)
        ktil, kdec, qtil, vtok, gs = [], [], [], [], []
        for _ in range(2):
            for lst in (ktil, kdec, qtil, vtok, gs):
                lst.append(MB.alloc([512], BF16))
            if _ == 0:
                off_dbl = MB.p
        kqT = MB.alloc([1024], BF16)
        sMT = MB.alloc([512], BF16)
        o_sb = MB.alloc([512], F32)
        ro = MB.alloc([512], BF16)
        bst = MB.alloc([4, 6], F32)
        mv = MB.alloc([4, 2], F32)
        rs4 = MB.alloc([4], F32)
        nmr = MB.alloc([4], F32)
        Sin = [MB.alloc([4, 128], F32), view(arena, off_krot, [4, 128], F32)]
        Sgs = [MB.alloc([4, 128], BF16), view(arena, off_qrot, [4, 128], BF16)]
        qm = [MB.alloc([4, 128], BF16), view(arena, off_qrot + 1024, [4, 128], BF16)]
        vm = [MB.alloc([4, 128], BF16), view(arena, off_rtA, [4, 128], BF16)]
        Snew = [view(arena, off_dbl, [4, 128], F32), view(arena, off_dbl + 2048, [4, 128], F32)]

        def lhs_act(c):
            return lambda k: actT[:, k, c * 128:(c + 1) * 128]

        def v4(ap):
            return ap[:, :].rearrange("p (h d) -> p h d", h=4)

        def mix_group(G):
            hk, hv, hq, hg = acquire(), acquire(), acquire(), acquire()
            hs = slice(G * 4, G * 4 + 4)
            reg_off = OFF_BIG + (G + 1) * 4 * T * 2
            nbuf = ((16 - (G + 1) * 4) * T * 2) // 2048
            nox = 6 if G == 0 else 4
            npre = nbuf - nox
            preS = [view(arena, reg_off + i_ * 2048, [4, 128], F32) for i_ in range(npre)]
            SnewX = [view(arena, reg_off + (npre + i_) * 2048, [4, 128], F32) for i_ in range(nox)]
            def stage_states():
                for t_ in range(npre):
                    hh_, s2_ = divmod(t_, 4)
                    S.dma("sp", preS[t_], st[s2_ * 4:s2_ * 4 + 4, G * 4 + hh_].rearrange("s d e -> d s e"),
                          "sinP%d" % G, writes=(["mixS_all"] if t_ == 0 else []) + [("mixS", t_)])
                S.retoken([("mixS", t_) for t_ in range(npre)] + ["mixS_all"], "sinP%d" % G)

            def tyof(c):
                return 0 if c < 8 else 1

            def P_k(c):
                proj(0, lhs_act(c), hk, [("actT", c)])

            def P_v(c):
                proj(1, lhs_act(c), hv, [("actT", c)])

            def P_q(c):
                proj(2, lhs_act(c), hq, [("actT", c)])

            def P_g(c):
                proj(3, lhs_act(c), hg, [("actT", c)])

            def E_k(c):
                ty, b = tyof(c), c % 2
                rotary(0, tabm, c, krot, "krot", rtA, rtB)
                kr3 = krot[:, :, :, :].rearrange("p h t d -> p h (t d)")
                A("dve", lambda e: e.tensor_tensor(out=v4(ktil[b]), in0=kr3,
                                                   in1=bc(tsc[:, 3 * ty + 1, hs].unsqueeze(2), [128, 4, 128]), op=ALU.mult),
                  reads=["krot", "tsc"], writes=[("ktil", b)])
                A("dve", lambda e: e.tensor_tensor(out=v4(kdec[b]), in0=kr3,
                                                   in1=bc(tsc[:, 3 * ty + 2, hs].unsqueeze(2), [128, 4, 128]), op=ALU.mult),
                  reads=["krot", "tsc"], writes=[("kdec", b)])

            def E_v(c):
                b = c % 2
                A("act", lambda e: e.activation(out=vtok[b], in_=pb[1][:, :], func=AF.Copy),
                  reads=[("pb", 1)], writes=[("vtok", b)])

            def E_q(c):
                ty, b = tyof(c), c % 2
                rotary(2, tabm, c, qrot, "qrot", rtA, rtB)
                A("dve", lambda e: e.tensor_tensor(out=v4(qtil[b]),
                                                   in0=qrot[:, :, :, :].rearrange("p h t d -> p h (t d)"),
                                                   in1=bc(tsc[:, 3 * ty + 0, hs].unsqueeze(2), [128, 4, 128]), op=ALU.mult),
                  reads=["qrot", "tsc"], writes=[("qtil", b)])

            def E_g(c):
                b = c % 2
                A("act", lambda e: e.activation(out=gs[b], in_=pb[3][:, :], func=AF.Silu),
                  reads=[("pb", 3)], writes=[("gs", b)])

            def C1(c):
                b = c % 2
                for hh in range(4):
                    A("pe", lambda e, hh=hh: e.transpose(pb7[:, hh * 128:(hh + 1) * 128],
                                                         ktil[b][:, hh * 128:(hh + 1) * 128], ident_b),
                      reads=[("ktil", b), "ident_b"], writes=["pb7"])
                for hh in range(4):
                    A("pe", lambda e, hh=hh: e.transpose(pb7[:, 512 + hh * 128:512 + (hh + 1) * 128],
                                                         qtil[b][:, hh * 128:(hh + 1) * 128], ident_b),
                      reads=[("qtil", b), "ident_b"], writes=["pb7"])
                A("act", lambda e: e.activation(out=kqT, in_=pb7[:, :], func=AF.Copy), reads=["pb7"], writes=["kqT"])

            def C2(c):
                ty = tyof(c)
                for hh in range(4):
                    A("pe", lambda e, hh=hh: e.matmul(pb[4][:, hh * 128:(hh + 1) * 128],
                                                      lhsT=kqT[:, hh * 128:(hh + 1) * 128],
                                                      rhs=kqT[:, 512 + hh * 128:512 + (hh + 1) * 128],
                                                      start=True, stop=True),
                      reads=["kqT"], writes=[("pb", 4)])
                A("dve", lambda e: e.tensor_tensor(
                    out=v4(sMT), in0=v4(pb[4]), in1=bc(caus[:, ty:ty + 1, :], [128, 4, 128]), op=ALU.mult),
                  reads=[("pb", 4), "caus"], writes=["sMT"])

            def C3_prompt(c):
                b = c % 2
                for hh in range(4):
                    hd = G * 4 + hh
                    A("pe", lambda e, hh=hh: e.matmul(pb[5][:, hh * 128:(hh + 1) * 128],
                                                      lhsT=sMT[:, hh * 128:(hh + 1) * 128],
                                                      rhs=vtok[b][:, hh * 128:(hh + 1) * 128], start=True, stop=False),
                      reads=["sMT", ("vtok", b)], writes=[("pb", 5)], sig=False)
                    A("pe", lambda e, hh=hh, hd=hd: e.matmul(pb[5][:, hh * 128:(hh + 1) * 128],
                                                             lhsT=kqT[:, 512 + hh * 128:512 + (hh + 1) * 128],
                                                             rhs=Sg_b[:, hd, :], start=False, stop=True),
                      reads=["kqT", ("Sg_b", hd)], writes=[("pb", 5)])
                for hh in range(4):
                    A("pe", lambda e, hh=hh: e.matmul(pb[6][:, hh * 128:(hh + 1) * 128],
                                                      lhsT=kdec[b][:, hh * 128:(hh + 1) * 128],
                                                      rhs=vtok[b][:, hh * 128:(hh + 1) * 128], start=True, stop=True),
                      reads=[("kdec", b), ("vtok", b)], writes=[("pb", 6)])

            def C3_state(c):
                for hh in range(4):
                    hd = G * 4 + hh
                    A("dve", lambda e, hh=hh, hd=hd: e.scalar_tensor_tensor(
                        out=S_f[:, hd, :], in0=S_f[:, hd, :], scalar=float(GAM[hd] ** 128),
                        in1=pb[6][:, hh * 128:(hh + 1) * 128], op0=ALU.mult, op1=ALU.add),
                      reads=[("S_f", hd), ("pb", 6)], writes=[("S_f", hd)])
                    A("act", lambda e, hd=hd: e.activation(out=Sg_b[:, hd, :], in_=S_f[:, hd, :], func=AF.Copy,
                                                           scale=float(GAM[hd])),
                      reads=[("S_f", hd)], writes=[("Sg_b", hd)])
                if c == 7:
                    S.dma("sp", sp_o[G * 4:(G + 1) * 4].rearrange("h d e -> d h e"), S_f[:, G * 4:(G + 1) * 4, :],
                          "spo%d" % G, reads=[("S_f", G * 4 + i_) for i_ in range(4)], writes=[("spo", G)], is_out=True)

            def C3_sample(c):
                b = c % 2
                SinB = [Sin[0], Sin[1], o_sb[:, :].rearrange("p (s e) -> p s e", s=4), S_f[:, G * 4:(G + 1) * 4, :]]
                SnewB = [Snew[0], Snew[1]] + SnewX
                nsn = len(SnewB)
                fence_keys = ["krot", "qrot", "rtA", ("Sin", 1), ("Sin", 2), ("Sin", 3), ("Sgs", 1), ("qm", 1), ("vm", 1),
                              ("ktil", 1), ("kdec", 1), ("qtil", 1), ("vtok", 1), ("Snew", 0), ("Snew", 1), "o_sb"] + \
                             [("S_f", G * 4 + i_) for i_ in range(4)]
                A("dve", lambda e: e.memset(stat[:, 40:41], 0.0), writes=fence_keys)

                def src_of(t):
                    if t < npre:
                        return preS[t], ("mixS", t)
                    return SinB[t % 4], ("Sin", t % 4)

                def load_in(t):
                    if t < npre or t >= 16:
                        return
                    hh_, s2_ = divmod(t, 4)
                    j_ = t % 4
                    S.dma("sp", SinB[j_], st[s2_ * 4:s2_ * 4 + 4, G * 4 + hh_].rearrange("s d e -> d s e"),
                          "sin%d" % j_, writes=[("Sin", j_)])

                for t0 in range(npre, min(16, npre + 3)):
                    load_in(t0)
                for t in range(16):
                    hh, s2 = divmod(t, 4)
                    hd = G * 4 + hh
                    j = t % 2
                    jn = t % nsn
                    sin_ap, sin_key = src_of(t)
                    sq = slice(s2 * 4, s2 * 4 + 4)
                    if s2 == 0:
                        A("pe", lambda e, hh=hh: e.matmul(pb[5][:, hh * 128:(hh + 1) * 128],
                                                          lhsT=sMT[:, hh * 128:(hh + 1) * 128],
                                                          rhs=vtok[b][:, hh * 128:(hh + 1) * 128], start=True, stop=False),
                          reads=["sMT", ("vtok", b)], writes=[("pb", 5)], sig=False)
                    if t >= npre:
                        load_in(t + 3)
                    A("act", lambda e, hd=hd, j=j, sin_ap=sin_ap: e.activation(out=Sgs[j], in_=sin_ap, func=AF.Copy,
                                                                               scale=float(GAM[hd])),
                      reads=[sin_key, "mixS_all"], writes=[("Sgs", j)])
                    A("dve", lambda e, hh=hh, sq=sq, j=j: e.tensor_tensor(
                        out=qm[j], in0=bc(kqT[:, 512 + hh * 128:512 + (hh + 1) * 128].unsqueeze(1), [128, 4, 128]),
                        in1=bmask[:, sq, :], op=ALU.mult),
                      reads=["kqT", "bmask"], writes=[("qm", j)])
                    A("dve", lambda e, hh=hh, sq=sq, j=j: e.tensor_tensor(
                        out=vm[j], in0=bc(vtok[b][:, hh * 128:(hh + 1) * 128].unsqueeze(1), [128, 4, 128]),
                        in1=bc(tmask[:, sq].unsqueeze(2), [128, 4, 128]), op=ALU.mult),
                      reads=[("vtok", b), "tmask"], writes=[("vm", j)])
                    for sj in range(4):
                        last = (s2 == 3 and sj == 3)
                        A("pe", lambda e, hh=hh, sj=sj, last=last, j=j: e.matmul(
                            pb[5][:, hh * 128:(hh + 1) * 128], lhsT=qm[j][:, sj, :], rhs=Sgs[j][:, sj, :],
                            start=False, stop=last),
                          reads=[("qm", j), ("Sgs", j)], writes=[("pb", 5)], sig=last)
                    bk = 6 if j == 0 else 4
                    A("pe", lambda e, hh=hh, bk=bk, j=j: e.matmul(
                        pb[bk][:, :], lhsT=kdec[b][:, hh * 128:(hh + 1) * 128],
                        rhs=vm[j][:, :, :].rearrange("p s e -> p (s e)"), start=True, stop=True),
                      reads=[("kdec", b), ("vm", j)], writes=[("pb", bk)])
                    A("dve", lambda e, bk=bk, hd=hd, jn=jn, sin_ap=sin_ap: e.scalar_tensor_tensor(
                        out=SnewB[jn], in0=sin_ap, scalar=float(GAM[hd] ** 8),
                        in1=pb[bk][:, :].rearrange("p (s e) -> p s e", s=4), op0=ALU.mult, op1=ALU.add),
                      reads=[sin_key, ("pb", bk), "mixS_all"], writes=[("Snew", jn)])
                    S.dma("sp", ss_o[sq, hd].rearrange("s d e -> d s e"), SnewB[jn], "sso%d" % jn,
                          reads=[("Snew", jn), "mixS_all"], writes=[("sso", hd, s2)], is_out=True)
                A("dve", lambda e: e.memset(stat[:, 40:41], 0.0), writes=fence_keys)

            def C3_norm(c):
                b = c % 2
                for hh in range(4):
                    A("dve", lambda e, hh=hh: e.bn_stats(out=bst[:, hh, :], in_=pb[5][:, hh * 128:(hh + 1) * 128]),
                      reads=[("pb", 5)], writes=["bst"])
                for hh in range(4):
                    A("dve", lambda e, hh=hh: e.bn_aggr(out=mv[:, hh, :], in_=bst[:, hh, :]), reads=["bst"], writes=["mv"])
                rstd_from_ss(mv[:, :, 1], rs4, 1.0, ["mv"], "rs4")
                A("dve", lambda e: e.scalar_tensor_tensor(out=nmr, in0=mv[:, :, 0], scalar=-1.0, in1=rs4,
                                                          op0=ALU.mult, op1=ALU.mult),
                  reads=["mv", "rs4"], writes=["nmr"])
                for hh in range(4):
                    A("act", lambda e, hh=hh: e.activation(
                        out=o_sb[:, hh * 128:(hh + 1) * 128], in_=pb[5][:, hh * 128:(hh + 1) * 128], func=AF.Identity,
                        scale=rs4[:, hh:hh + 1], bias=nmr[:, hh:hh + 1]),
                      reads=[("pb", 5), "rs4", "nmr"], writes=["o_sb"])
                A("dve", lambda e: e.tensor_tensor(out=ro, in0=o_sb, in1=gs[b], op=ALU.mult),
                  reads=["o_sb", ("gs", b)], writes=["ro"])

            def C4(c):
                for hh in range(4):
                    A("pe", lambda e, hh=hh: e.transpose(pb7[:, hh * 128:(hh + 1) * 128],
                                                         ro[:, hh * 128:(hh + 1) * 128], ident_b),
                      reads=["ro", "ident_b"], writes=["pb7"])
                A("act", lambda e: e.activation(
                    out=mixT[:, G * 4:(G + 1) * 4, c * 128:(c + 1) * 128], in_=v4(pb7[:, 0:512]), func=AF.Copy),
                  reads=["pb7"], writes=[("mixT", c)] + (["mixS_all"] if G == 1 else []))

            P_k(0); E_k(0); P_v(0); E_v(0); P_q(0); E_q(0); P_g(0); E_g(0)
            for c in range(8):
                n = c + 1
                if c == 2:
                    stage_states()
                P_k(n)
                if n == 8:
                    release(hk)
                if c > 0:
                    mod_b(8 + G * 8 + c - 1, tbank=4)
                    C4(c - 1)
                C1(c)
                if c > 0:
                    E_g(c)
                E_k(n)
                P_v(n)
                if n == 8:
                    release(hv)
                E_v(n)
                C2(c)
                P_q(n)
                if n == 8:
                    release(hq)
                E_q(n)
                C3_prompt(c)
                C3_norm(c)
                C3_state(c)
                mod_a(8 + G * 8 + c)
                P_g(n)
                if n == 8:
                    release(hg)
            mod_b(8 + G * 8 + 7, tbank=4)
            C4(7)
            C1(8)
            E_g(8)
            C2(8)
            C3_sample(8)
            C3_norm(8)
            C4(8)

        for G in range(2):
            mix_group(G)

        S.barrier()
        MB.reset()
        wsT = MB.alloc([2, 8, 128], BF16)
        lng = MB.alloc([1024], F32)
        lnb = MB.alloc([1024], F32)
        bst2 = MB.alloc([2, 6], F32)
        mv2 = MB.alloc([2], F32)
        rs1 = MB.alloc([1], F32)
        mark = MB.p
        wtmp = MB.alloc([2, 8, 128], F32)
        S.dma("sp", lng, bc(lngb[0:1, :], [128, 1024]), "misc1", writes=["lng"])
        S.dma("sp", lnb, bc(lngb[1:2, :], [128, 1024]), "misc2", writes=["lnb"])
        S.dma("sp", wtmp, wsT_d, "misc0", writes=["wtmp"])
        for ty in range(2):
            A("dve", lambda e, ty=ty: e.tensor_tensor(out=wsT[:, ty, :, :], in0=wtmp[:, ty, :, :],
                                                      in1=bc(caus[:, ty:ty + 1, :], [128, 8, 128]), op=ALU.mult),
              reads=["wtmp", "caus"], writes=["wsT"])
        S.barrier()
        MB.p = mark
        gv = MB.alloc([1024], F32)
        vnb = MB.alloc([1024], BF16)
        gu = MB.alloc([1024], F32)
        so = MB.alloc([1024], BF16)
        hv0, hv1, hu0, hu1 = acquire(), acquire(), acquire(), acquire()

        def Pvs(c):
            proj(0, lhs_act(c), hv0, [("actT", c)])
            proj(1, lhs_act(c), hv1, [("actT", c)])

        def Pu(c):
            proj(4, lhs_act(c), hu0, [("actT", c)])
            proj(5, lhs_act(c), hu1, [("actT", c)])

        def LNc(c):
            for i in range(2):
                A("act", lambda e, i=i: e.activation(out=gv[:, i * 512:(i + 1) * 512], in_=pb[i][:, :], func=AF.Gelu),
                  reads=[("pb", i)], writes=["gv"])
            for i in range(2):
                A("dve", lambda e, i=i: e.bn_stats(out=bst2[:, i, :], in_=gv[:, i * 512:(i + 1) * 512]),
                  reads=["gv"], writes=["bst2"])
            A("dve", lambda e: e.bn_aggr(out=mv2, in_=bst2[:, :, :].rearrange("p a b -> p (a b)")),
              reads=["bst2"], writes=["mv2"])
            rstd_from_ss(mv2[:, 1:2], rs1, 1.0, ["mv2"], "rs1")
            A("dve", lambda e: e.tensor_scalar(out=gv, in0=gv, scalar1=mv2[:, 0:1], scalar2=rs1[:, 0:1],
                                               op0=ALU.subtract, op1=ALU.mult), reads=["gv", "mv2", "rs1"], writes=["gv"])
            A("dve", lambda e: e.tensor_tensor(out=gv, in0=gv, in1=lng, op=ALU.mult), reads=["gv", "lng"], writes=["gv"])
            A("dve", lambda e: e.tensor_tensor(out=gv, in0=gv, in1=lnb, op=ALU.add), reads=["gv", "lnb"], writes=["gv"])
            A("act", lambda e: e.activation(out=vnb, in_=gv, func=AF.Copy), reads=["gv"], writes=["vnb"])
            if c == 8:
                S.dma("sp", vn_o, gv, "vno", reads=["gv"], writes=["vno"], is_out=True)

        def SPc(c):
            ty = 0 if c < 8 else 1
            for g in range(8):
                bk = 2 + g // 4
                A("pe", lambda e, g=g, bk=bk, ty=ty: e.matmul(pb[bk][:, (g % 4) * 128:(g % 4 + 1) * 128],
                                                              lhsT=wsT[:, ty, g, :], rhs=vnb[:, g * 128:(g + 1) * 128],
                                                              start=True, stop=True),
                  reads=["wsT", "vnb"], writes=[("pb", bk)])

        def GUc(c):
            for i in range(2):
                A("act", lambda e, i=i: e.activation(out=gu[:, i * 512:(i + 1) * 512], in_=pb[4 + i][:, :], func=AF.Gelu),
                  reads=[("pb", 4 + i)], writes=["gu"])

        def STc(c):
            ty = 0 if c < 8 else 1
            for g in range(8):
                bk = 2 + g // 4
                A("dve", lambda e, g=g, bk=bk, ty=ty: e.scalar_tensor_tensor(
                    out=so[:, g * 128:(g + 1) * 128], in0=pb[bk][:, (g % 4) * 128:(g % 4 + 1) * 128],
                    scalar=bcol[:, ty, g:g + 1], in1=gu[:, g * 128:(g + 1) * 128], op0=ALU.add, op1=ALU.mult),
                  reads=[("pb", bk), "bcol", "gu"], writes=["so"])

        def Tc(c):
            for g in range(8):
                A("pe", lambda e, g=g: e.transpose(pb7[:, g * 128:(g + 1) * 128], so[:, g * 128:(g + 1) * 128], ident_b),
                  reads=["so", "ident_b"], writes=["pb7"])
            A("act", lambda e, c=c: e.activation(out=mixT[:, 8:16, c * 128:(c + 1) * 128],
                                                 in_=pb7[:, :].rearrange("p (h n) -> p h n", h=8), func=AF.Copy),
              reads=["pb7"], writes=[("mixT", c), "mixS_all"])

        Pvs(0); LNc(0); Pu(0); GUc(0)
        for c in range(NCH):
            n = c + 1
            if n < NCH:
                Pvs(n)
                if n == NCH - 1:
                    release(hv0)
                    release(hv1)
            SPc(c)
            STc(c)
            if n < NCH:
                LNc(n)
                Pu(n)
                if n == NCH - 1:
                    release(hu0)
                    release(hu1)
            if n < NCH:
                GUc(n)
            Tc(c)
        S.barrier()

        PB3 = Bump(arena, OFF_P3TMP, ARENA_B)
        xin = PB3.alloc([D], F32)
        xn = PB3.alloc([D], F32)
        gtgP = PB3.alloc([D], F32)
        gtgS = PB3.alloc([D], F32)
        junk = PB3.alloc([512], BF16)
        xn2 = PB3.alloc([D], F32)
        bst3 = PB3.alloc([4, 6], F32)
        mv3 = PB3.alloc([2], F32)
        ss4 = stat[:, 8:12]

        def load_gtg(which, ty, tile, key):
            if ty == 0:
                S.dma("sp", tile, bc(ggd[which, 0:1, :], [128, D]), "g_" + key, reads=[("ggd", 2 + 3 * which)], writes=[key])
            else:
                for s_ in range(16):
                    S.dma("sp", tile[8 * s_:8 * s_ + 8, :], bc(ggd[which, 1 + s_:2 + s_, :], [8, D]), "g_" + key,
                          reads=[("ggd", 2 + 3 * which)], writes=[key if s_ == 0 else (key, s_)])
            S.retoken([key], "g_" + key)

        def post_norm_residual(src_fn, src_keys, xres, xres_key, gt, gt_key, tmp_fn, jk):
            for nb in range(4):
                A("act", lambda e, nb=nb: e.activation(out=jk, in_=src_fn(nb), func=AF.Square,
                                                       accum_out=ss4[:, nb:nb + 1]),
                  reads=[src_keys[nb], "ss4"], writes=["junk", "ss4"])
            A("dve", lambda e: e.tensor_reduce(out=ss, in_=ss4, axis=AX.X, op=ALU.add), reads=["ss4"], writes=["ss"])
            rstd_from_ss(ss, rs, float(D), ["ss"], "rs")
            for nb in range(4):
                sl = slice(nb * 512, (nb + 1) * 512)
                t_ap, t_key = tmp_fn(nb)
                A("dve", lambda e, nb=nb, sl=sl, t_ap=t_ap: e.scalar_tensor_tensor(
                    out=t_ap, in0=src_fn(nb), scalar=rs, in1=gt[:, sl], op0=ALU.mult, op1=ALU.mult),
                  reads=[src_keys[nb], "rs", gt_key], writes=[t_key])
                A("dve", lambda e, sl=sl, t_ap=t_ap: e.tensor_tensor(out=xres[:, sl], in0=xres[:, sl], in1=t_ap, op=ALU.add),
                  reads=[t_key, xres_key], writes=[xres_key])

        how = [acquire() for _ in range(4)]
        load_gtg(0, 0, gtgP, "gtgP")
        load_gtg(0, 1, gtgS, "gtgS")
        xnb = junk_region = None
        xnb = view(arena, OFF_P3TMP + 8192, [D], BF16)
        pb6b = pb[6][:, :].bitcast(BF16)
        mkeys = [("pb", 0), ("pb", 1), ("pb", 2), ("pb", 3)]

        def wo_proj(c):
            for nb in range(4):
                proj(nb, (lambda c: (lambda k: mixT[:, k, c * 128:(c + 1) * 128]))(c), how[nb], [("mixT", c)])

        def wo_part1(c):
            for nb in range(4):
                A("act", lambda e, nb=nb: e.activation(out=junk, in_=pb[nb][:, :], func=AF.Square,
                                                       accum_out=ss4[:, nb:nb + 1]),
                  reads=[mkeys[nb], "ss4"], writes=["junk", "ss4"])

        def wo_part2a(c):
            gt, gt_key = (gtgP, "gtgP") if c < 8 else (gtgS, "gtgS")
            S.dma("sp", xin, xm[c * 128:(c + 1) * 128, :], "xin", writes=["xin"])
            A("dve", lambda e: e.tensor_reduce(out=ss, in_=ss4, axis=AX.X, op=ALU.add), reads=["ss4"], writes=["ss"])
            rstd_from_ss(ss, rs, float(D), ["ss"], "rs")
            for nb in range(4):
                sl = slice(nb * 512, (nb + 1) * 512)
                A("dve", lambda e, nb=nb, sl=sl: e.scalar_tensor_tensor(
                    out=xn2[:, sl], in0=pb[nb][:, :], scalar=rs, in1=gt[:, sl], op0=ALU.mult, op1=ALU.mult),
                  reads=[mkeys[nb], "rs", gt_key], writes=[("xn2", nb)])

        def wo_part2b(c):
            for nb in range(4):
                sl = slice(nb * 512, (nb + 1) * 512)
                A("dve", lambda e, sl=sl: e.tensor_tensor(out=xin[:, sl], in0=xin[:, sl], in1=xn2[:, sl], op=ALU.add),
                  reads=[("xn2", nb), "xin"], writes=["xin"])
            S.dma("sp", x1s[c * 128:(c + 1) * 128, :], xin, "x1st", reads=["xin"], writes=[("x1s", c)])
            A("act", lambda e: e.activation(out=xn2, in_=xin, func=AF.Square, accum_out=ss2),
              reads=["xin"] + [("xn2", nb_) for nb_ in range(4)], writes=[("xn2", nb_) for nb_ in range(4)] + ["ss2"])
            rstd_from_ss(ss2, rs2, float(D), ["ss2"], "rs2")
            A("act", lambda e: e.activation(out=xnb[:, 0:1024], in_=xin[:, 0:1024], func=AF.Copy, scale=rs2),
              reads=["xin", "rs2"], writes=["xnb"])
            A("dve", lambda e: e.tensor_scalar(out=xnb[:, 1024:2048], in0=xin[:, 1024:2048], scalar1=rs2, scalar2=None,
                                               op0=ALU.mult),
              reads=["xin", "rs2"], writes=[("xnb", 1)])

        def wo_T(c):
            for k in range(16):
                dst_ps = pb7 if k < 8 else pb6b
                A("pe", lambda e, k=k, dst_ps=dst_ps: e.transpose(dst_ps[:, (k % 8) * 128:(k % 8 + 1) * 128],
                                                                  xnb[:, k * 128:(k + 1) * 128], ident_b),
                  reads=["xnb", ("xnb", 1), "ident_b"], writes=["pb7" if k < 8 else ("pb", 6)])

        def wo_evac(c):
            ty = 0 if c < 8 else 1
            aT2, bT2 = abT[3], abT[2]
            for k in range(16):
                dkeys = [("actT", c)] if k < 8 else [("actTb", c)]
                src_ps, skey = (pb7, "pb7") if k < 8 else (pb6b, ("pb", 6))
                pv1 = src_ps[:, (k % 8) * 128:(k % 8 + 1) * 128]
                dv1 = actT[:, k, c * 128:(c + 1) * 128]
                if ty == 0 and k < 8:
                    A("act", lambda e, k=k, pv1=pv1, dv1=dv1: e.activation(
                        out=dv1, in_=pv1, func=AF.Identity, scale=aT2[:, k, 0:1], bias=bT2[:, k, 0:1]),
                      reads=[skey, "abT"], writes=dkeys)
                elif ty == 0:
                    A("dve", lambda e, k=k, pv1=pv1, dv1=dv1: e.tensor_scalar(
                        out=dv1, in0=pv1, scalar1=aT2[:, k, 0:1], scalar2=bT2[:, k, 0:1], op0=ALU.mult, op1=ALU.add),
                      reads=[skey, "abT"], writes=dkeys)
                else:
                    pv3 = pv1.rearrange("p (s j) -> p s j", s=16)
                    dv3 = dv1.rearrange("p (s j) -> p s j", s=16)
                    A("dve", lambda e, k=k, pv3=pv3, dv3=dv3: e.tensor_tensor(
                        out=dv3, in0=pv3, in1=bc(aT2[:, k, 1:17].unsqueeze(2), [128, 16, 8]), op=ALU.mult),
                      reads=[skey, "abT"], writes=dkeys)
                    A("dve", lambda e, k=k, dv3=dv3: e.tensor_tensor(
                        out=dv3, in0=dv3, in1=bc(bT2[:, k, 1:17].unsqueeze(2), [128, 16, 8]), op=ALU.add),
                      reads=["abT"] + dkeys, writes=dkeys)

        ss2 = stat[:, 12:13]
        rs2 = stat[:, 13:14]
        wo_proj(0)
        for c in range(NCH):
            wo_part1(c)
            wo_part2a(c)
            if c > 0:
                wo_T(c - 1)
            if c + 1 < NCH:
                wo_proj(c + 1)
                if c + 1 == NCH - 1:
                    for h_ in how:
                        release(h_)
            if c > 0:
                wo_evac(c - 1)
            wo_part2b(c)
        wo_T(NCH - 1)
        wo_evac(NCH - 1)
        S.barrier()

        OFF_FIN = OFF_P3TMP + 2 * 8 * 640 * 2 + 2 * 320 * 4
        FB = Bump(arena, OFF_FIN, ARENA_B)
        xinF = [FB.alloc([D], F32) for _ in range(2)]
        gtgF = FB.alloc([D], F32)
        junkF = pb[1][:, 0:512]
        half_chunks = [list(range(0, 5)), list(range(5, 9))]

        def final_load(g):
            S.dma("sp", xinF[g % 2], x1s[g * 128:(g + 1) * 128, :], "xinF%d" % (g % 2),
                  reads=[("x1s", g)], writes=[("xinF", g % 2)])

        def final_chunk(half, cl, facc, gt, gt_key):
            c = half_chunks[half][cl]
            xb, xkey = xinF[c % 2], ("xinF", c % 2)
            fk = ("facc", cl)
            for nb in range(4):
                A("act", lambda e, nb=nb: e.activation(out=junkF, in_=facc[:, cl, nb * 512:(nb + 1) * 512], func=AF.Square,
                                                       accum_out=ss4[:, nb:nb + 1]),
                  reads=[fk, "ss4"], writes=[("pb", 1), "ss4"])
            A("dve", lambda e: e.tensor_reduce(out=ss, in_=ss4, axis=AX.X, op=ALU.add), reads=["ss4"], writes=["ss"])
            rstd_from_ss(ss, rs, float(D), ["ss"], "rs")
            A("dve", lambda e: e.scalar_tensor_tensor(out=facc[:, cl, :], in0=facc[:, cl, :], scalar=rs, in1=gt,
                                                      op0=ALU.mult, op1=ALU.mult),
              reads=[fk, "rs", gt_key], writes=[fk])
            A("dve", lambda e: e.tensor_tensor(out=xb, in0=xb, in1=facc[:, cl, :], op=ALU.add),
              reads=[fk, xkey], writes=[xkey])
            S.dma("sp", y_o[c * 128:(c + 1) * 128, :], xb, "yst%d" % (c % 2), reads=[xkey], writes=[("y", c)],
                  is_out=True)
            if c + 2 < NCH:
                final_load(c + 2)

        load_gtg(1, 0, gtgF, "gtgF")
        final_load(0)
        final_load(1)
        facc_prev = None
        for half in range(2):
            PB3.reset()
            chunks = half_chunks[half]
            nck = len(chunks)
            tok0 = chunks[0] * 128
            Th = nck * 128
            ntb = 1 if Th <= 512 else 2
            Tb = Th // ntb
            facc = view(arena, OFF_BIG, [nck, D], F32)
            aT = [PB3.alloc([8, Th], BF16) for _ in range(2)]
            rtmp = [PB3.alloc([Tb], F32) for _ in range(2)]
            assert PB3.p <= OFF_FIN
            rot = 0
            fin_after = {1: 0, 3: 1, 5: 2, 6: 3, 7: 4}
            for gk in range(8):
                aTg = aT[gk % 2]
                akey = ("aT", gk % 2)
                for pj in range(2):
                    h = acquire()
                    for jj in range(4):
                        j = pj * 4 + jj
                        b0 = (j % 2) * 2
                        for k in range(16):
                            for tb in range(ntb):
                                A("pe", lambda e, k=k, tb=tb, b0=b0, jj=jj, h=h, Tb=Tb, tok0=tok0: e.matmul(
                                    pb[b0 + tb][:, 0:Tb], lhsT=h[1][:, k, jj * 128:(jj + 1) * 128],
                                    rhs=actT[:, k, tok0 + tb * Tb: tok0 + (tb + 1) * Tb],
                                    start=(k == 0), stop=(k == 15)),
                                  reads=[h[2]] + [("actT", cc_) for cc_ in chunks] + [("actTb", cc_) for cc_ in chunks],
                                  writes=[("pb", b0 + tb)],
                                  sig=(k == 15))
                        for tb in range(ntb):
                            rt = rtmp[(j * ntb + tb) % 2]
                            rkey = ("rtmp", (j * ntb + tb) % 2)
                            A("act", lambda e, tb=tb, b0=b0, Tb=Tb, rt=rt: e.activation(out=rt, in_=pb[b0 + tb][:, 0:Tb], func=AF.Relu),
                              reads=[("pb", b0 + tb)], writes=[rkey])
                            A("dve", lambda e, tb=tb, j=j, aTg=aTg, Tb=Tb, rt=rt: e.tensor_tensor(
                                out=aTg[:, j, tb * Tb:(tb + 1) * Tb], in0=rt, in1=rt, op=ALU.mult),
                              reads=[rkey], writes=[akey])
                        if half == 1 and gk == 0 and j in fin_after:
                            final_chunk(0, fin_after[j], facc_prev, gtgF, "gtgF")
                    release(h)
                for nb in range(4):
                    h = acquire()
                    for cl in range(nck):
                        bk = 4 + rot % 3
                        rot += 1
                        for kk in range(8):
                            A("pe", lambda e, kk=kk, cl=cl, bk=bk, h=h, aTg=aTg: e.matmul(
                                pb[bk][:, :], lhsT=aTg[:, kk, cl * 128:(cl + 1) * 128], rhs=h[1][:, kk, :],
                                start=(kk == 0), stop=(kk == 7)),
                              reads=[h[2], akey], writes=[("pb", bk)], sig=(kk == 7))
                        fv = facc[:, cl, nb * 512:(nb + 1) * 512]
                        if gk == 0:
                            A("act", lambda e, fv=fv, bk=bk: e.activation(out=fv, in_=pb[bk][:, :], func=AF.Copy),
                              reads=[("pb", bk)], writes=[("facc", cl)])
                        else:
                            A("dve", lambda e, fv=fv, bk=bk: e.tensor_tensor(out=fv, in0=fv, in1=pb[bk][:, :], op=ALU.add),
                              reads=[("pb", bk), ("facc", cl)], writes=[("facc", cl)])
                    release(h)
            facc_prev = facc
        S.barrier()
        PB3.reset()
        gtgS2 = PB3.alloc([D], F32)
        load_gtg(1, 1, gtgS2, "gtgS2")
        for cl, c in enumerate(half_chunks[1]):
            final_chunk(1, cl, facc_prev, gtgS2 if c == 8 else gtgF, "gtgS2" if c == 8 else "gtgF")
        S.barrier()

        S.finish()
        with nc.Block() as block:
            S.emit(block)
    return nc


def _consts(core):
    hf = core % 2
    inv = (1.0 / (10000.0 ** (np.arange(0, 128, 2, dtype=np.float32) / np.float32(128)))).astype(np.float32)
    gam = np.array(GAM, dtype=np.float64)
    p = np.arange(128)

    def tab(pos):
        ang = pos.astype(np.float32)[:, :, None] * inv[None, None, :]
        return np.stack([np.cos(ang), np.sin(ang)], axis=2).astype(np.float32)

    posm = np.zeros((128, NCH), dtype=np.int64)
    for c in range(8):
        posm[:, c] = hf * 1024 + c * 128 + p
    posm[:, 8] = 16384 + (p % 8)
    posp = np.zeros((128, 8), dtype=np.int64)
    for c in range(8):
        posp[:, c] = c * 128 + p
    dk = 128.0 ** -0.5
    tsc = np.zeros((128, 6, 8), dtype=np.float64)
    n = p[:, None].astype(np.float64)
    tsc[:, 0] = gam[None] ** n
    tsc[:, 1] = gam[None] ** (-n) * dk
    tsc[:, 2] = gam[None] ** (127.0 - n) * dk
    n8 = (p % 8)[:, None].astype(np.float64)
    tsc[:, 3] = gam[None] ** n8
    tsc[:, 4] = gam[None] ** (-n8) * dk
    tsc[:, 5] = gam[None] ** (7.0 - n8) * dk
    pdec = np.zeros((128, 8, 8), dtype=np.float64)
    if hf == 1:
        for c in range(8):
            pdec[:, c, :] = gam[None] ** (1023.0 - (c * 128 + n)) * dk
    caus = np.zeros((128, 2, 128), dtype=np.float32)
    m, nn = p[:, None], p[None, :]
    caus[:, 0] = (m <= nn)
    caus[:, 1] = (m // 8 == nn // 8) & (m % 8 <= nn % 8)
    bmask = np.zeros((128, 16, 128), dtype=np.float32)
    for s in range(16):
        bmask[:, s, 8 * s:8 * s + 8] = 1.0
    tmask = (p[:, None] // 8 == np.arange(16)[None, :]).astype(np.float32)
    return dict(tabm=tab(posm), tabp=tab(posp), tsc=tsc.astype(np.float32), pdec=pdec.astype(np.float32),
                caus=caus, bmask=bmask, tmask=tmask, ident=np.eye(128, dtype=np.float32))


_NC_CACHE = {}


def kernel(x_prompt, x_sample, state_ret, c_prompt, c_sample, w_ada, b_ada, g_pre_mix, g_post_mix,
           g_pre_ffn, g_post_ffn, w_in, w_s, b_s, ln_g, ln_b, w_o, w_ff1, w_ff2, _cores=None):
    f = lambda a: np.ascontiguousarray(np.asarray(a, dtype=np.float32))
    x_prompt, x_sample, state_ret = f(x_prompt), f(x_sample), f(state_ret)
    c_prompt, c_sample = f(c_prompt), f(c_sample)
    shared = dict(
        w_ada=f(w_ada)[0], b_ada=f(b_ada), w_in=f(w_in)[0], w_o=f(w_o)[0], w_ff1=f(w_ff1)[0], w_ff2=f(w_ff2)[0],
        gvec=np.ascontiguousarray(np.concatenate([f(g_pre_mix), f(g_post_mix), f(g_pre_ffn), f(g_post_ffn)], axis=0)),
        lngb=np.ascontiguousarray(np.concatenate([f(ln_g), f(ln_b)], axis=0)),
    )
    ws = f(w_s)[0]
    bs = f(b_s)[0]
    wsT = np.zeros((128, 2, 8, 128), dtype=np.float32)
    wsT[:, 0] = ws.transpose(2, 0, 1)
    blk = ws[:, :8, :8].transpose(2, 0, 1)
    for a in range(16):
        wsT[8 * a:8 * a + 8, 1, :, 8 * a:8 * a + 8] = blk
    bcol = np.zeros((128, 2, 8), dtype=np.float32)
    bcol[:, 0] = bs.T
    bcol[:, 1] = np.tile(bs[:, :8].T, (16, 1))
    shared["wsT"] = wsT
    shared["bcol"] = bcol

    cores = list(range(8)) if _cores is None else _cores
    in_maps = []
    for core in cores:
        b, hf = core // 2, core % 2
        xs = x_sample[16 * core:16 * core + 16].reshape(128, D)
        m = dict(shared)
        m["xm"] = np.ascontiguousarray(np.concatenate([x_prompt[b, hf * 1024:(hf + 1) * 1024], xs], axis=0))
        m["xp"] = np.ascontiguousarray(x_prompt[b, 0:1024])
        m["cc"] = np.ascontiguousarray(np.concatenate([c_prompt[b:b + 1], c_sample[16 * core:16 * core + 16]], axis=0))
        m["st"] = np.ascontiguousarray(state_ret[0, 16 * core:16 * core + 16])
        m.update(_consts(core))
        in_maps.append(m)

    if "nc" not in _NC_CACHE:
        _NC_CACHE["nc"] = build_program()
    nc = _NC_CACHE["nc"]
    res = run_bass_kernel_spmd(nc, in_maps, core_ids=list(range(len(cores))))
    outs = res.results
    if _cores is not None:
        return outs
    y_prompt = np.zeros((4, 2048, D), dtype=np.float32)
    y_sample = np.zeros((128, 8, D), dtype=np.float32)
    sp = np.zeros((1, 4, 8, 128, 128), dtype=np.float32)
    ssn = np.zeros((1, 128, 8, 128, 128), dtype=np.float32)
    vn = np.zeros((1, 128, 8, 1024), dtype=np.float32)
    for core in range(8):
        b, hf = core // 2, core % 2
        r = outs[core]
        y_prompt[b, hf * 1024:(hf + 1) * 1024] = r["y"][0:1024]
        y_sample[16 * core:16 * core + 16] = r["y"][1024:1152].reshape(16, 8, D)
        if hf == 1:
            sp[0, b] = r["sp_out"]
        ssn[0, 16 * core:16 * core + 16] = r["ss_out"]
        vn[0, 16 * core:16 * core + 16] = r["vn_out"].reshape(16, 8, 1024)
    return (y_prompt, y_sample, sp, ssn, vn)
```
